# Optimizing a Trainium2 kernel written in Bass

```python
import math
import jax, jax.numpy as jnp
from jax import lax
import numpy as np

D_MODEL = 1024
BATCH = 16
SEQ = 2048
DEPTH = 2

CHUNK = 64
Q_BLOCK = 128
N_A_LAYERS = DEPTH // 2
N_B_LAYERS = DEPTH - N_A_LAYERS
A_HEADS = 8
A_HEAD_DIM = D_MODEL // (2 * A_HEADS)
B_HEADS = 16
B_HEAD_DIM = D_MODEL // B_HEADS
LEFT_CHUNKS = 8
BAND_CHUNKS = LEFT_CHUNKS + 1
MAX_REL_DIST = 256
D_FF = -(-8 * D_MODEL // (3 * 256)) * 256
ROPE_THETA = 10000.0
LN_EPS = 1e-5
SUBLN_EPS = 1e-5
DEEPNORM_ALPHA = (2 * DEPTH) ** 0.25
DEEPNORM_BETA = (8 * DEPTH) ** -0.25

kernel_name = "yoco_diffattn_chunkband_deepnorm"


def layer_norm(x, g, b):
    xf = x.astype(jnp.float32)
    mu = jnp.mean(xf, axis=-1, keepdims=True)
    var = jnp.mean(jnp.square(xf - mu), axis=-1, keepdims=True)
    y = (xf - mu) * lax.rsqrt(var + LN_EPS) * g.astype(jnp.float32) + b.astype(jnp.float32)
    return y.astype(x.dtype)


def rms_norm(x, g):
    xf = x.astype(jnp.float32)
    y = xf * lax.rsqrt(jnp.mean(jnp.square(xf), axis=-1, keepdims=True) + SUBLN_EPS)
    return (y * g.astype(jnp.float32)).astype(x.dtype)


def rope_tables(seq, dim):
    inv = ROPE_THETA ** (-jnp.arange(0, dim, 2, dtype=jnp.float32) / dim)
    ang = jnp.arange(seq, dtype=jnp.float32)[:, None] * inv[None, :]
    return jnp.cos(ang), jnp.sin(ang)


def apply_rope(x, cos, sin):
    half = x.shape[-1] // 2
    x1, x2 = x[..., :half], x[..., half:]
    cos = cos.astype(x.dtype)
    sin = sin.astype(x.dtype)
    return jnp.concatenate([x1 * cos - x2 * sin, x1 * sin + x2 * cos], axis=-1)


def lambda_init_fn(layer_idx):
    return 0.8 - 0.6 * math.exp(-0.3 * layer_idx)


def diff_attention(x, w_qkv, lam_params, subln_g, w_o, lambda_init):
    b, s, _ = x.shape
    qkv = x @ w_qkv
    q, k, v = jnp.split(qkv, 3, axis=-1)
    q = q.reshape(b, s, A_HEADS, 2, A_HEAD_DIM)
    k = k.reshape(b, s, A_HEADS, 2, A_HEAD_DIM)
    v = v.reshape(b, s, A_HEADS, 2 * A_HEAD_DIM)
    cos, sin = rope_tables(s, A_HEAD_DIM)
    cos = cos[:, None, None, :]
    sin = sin[:, None, None, :]
    q = apply_rope(q, cos, sin) * (A_HEAD_DIM ** -0.5)
    k = apply_rope(k, cos, sin)
    lp = lam_params.astype(jnp.float32)
    lam = jnp.exp(jnp.sum(lp[0] * lp[1])) - jnp.exp(jnp.sum(lp[2] * lp[3])) + lambda_init
    chunk_id = jnp.arange(s) // CHUNK
    neg = jnp.finfo(jnp.float32).min
    outs = []
    for i in range(s // Q_BLOCK):
        q0 = i * Q_BLOCK
        kend = q0 + Q_BLOCK
        sc = jnp.einsum('bqhmd,bkhmd->bhmqk', q[:, q0:kend], k[:, :kend],
                        preferred_element_type=jnp.float32)
        allowed = chunk_id[None, :kend] <= chunk_id[q0:kend, None]
        p = jax.nn.softmax(jnp.where(allowed, sc, neg), axis=-1)
        a = p[:, :, 0] - lam * p[:, :, 1]
        outs.append(jnp.einsum('bhqk,bkhe->bqhe', a.astype(v.dtype), v[:, :kend]))
    o = jnp.concatenate(outs, axis=1)
    o = rms_norm(o, subln_g) * (1.0 - lambda_init)
    return o.reshape(b, s, D_MODEL) @ w_o


def shared_kv_band(x, kv_w):
    b, s, _ = x.shape
    nc = s // CHUNK
    kv = x @ kv_w
    k, v = jnp.split(kv, 2, axis=-1)
    k = k.reshape(b, nc, CHUNK, B_HEADS, B_HEAD_DIM)
    v = v.reshape(b, nc, CHUNK, B_HEADS, B_HEAD_DIM)
    pad = ((0, 0), (LEFT_CHUNKS, 0), (0, 0), (0, 0), (0, 0))
    kp = jnp.pad(k, pad)
    vp = jnp.pad(v, pad)
    k_band = jnp.concatenate([kp[:, j:j + nc] for j in range(BAND_CHUNKS)], axis=2)
    v_band = jnp.concatenate([vp[:, j:j + nc] for j in range(BAND_CHUNKS)], axis=2)
    band_chunk = jnp.arange(BAND_CHUNKS * CHUNK) // CHUNK
    band_valid = (jnp.arange(nc)[:, None] - LEFT_CHUNKS + band_chunk[None, :]) >= 0
    return k_band, v_band, band_valid


def chunk_attention(x, k_band, v_band, band_valid, w_q, rel_table, w_o):
    b, s, _ = x.shape
    nc = s // CHUNK
    q = (x @ w_q).reshape(b, nc, CHUNK, B_HEADS, B_HEAD_DIM) * (B_HEAD_DIM ** -0.5)
    rel = (jnp.arange(BAND_CHUNKS * CHUNK)[None, :] - LEFT_CHUNKS * CHUNK
           - jnp.arange(CHUNK)[:, None])
    idx = jnp.clip(rel, -MAX_REL_DIST, MAX_REL_DIST) + MAX_REL_DIST
    bias = rel_table[:, idx].astype(jnp.float32)
    sc = jnp.einsum('bnqhd,bnkhd->bhnqk', q, k_band,
                    preferred_element_type=jnp.float32) + bias[:, None]
    sc = jnp.where(band_valid[:, None, :], sc, jnp.finfo(jnp.float32).min)
    p = jax.nn.softmax(sc, axis=-1)
    o = jnp.einsum('bhnqk,bnkhd->bnqhd', p.astype(v_band.dtype), v_band)
    return o.reshape(b, s, D_MODEL) @ w_o


def swiglu(x, w_in, w_out):
    gate, up = jnp.split(x @ w_in, 2, axis=-1)
    return (jax.nn.silu(gate) * up) @ w_out


def setup_inputs(seed: int = 0) -> dict:
    key = jax.random.key(seed)
    ks = jax.random.split(key, 14)
    f32 = jnp.float32
    d = D_MODEL
    nrm = lambda k, shape, sc: jax.random.normal(k, shape, f32) * sc
    return {
        "x": nrm(ks[0], (BATCH, SEQ, d), 1.0),
        "a_w_qkv": nrm(ks[1], (N_A_LAYERS, d, 3 * d), d ** -0.5),
        "a_lambda": nrm(ks[2], (N_A_LAYERS, 4, A_HEAD_DIM), 0.1),
        "a_subln_g": 1.0 + nrm(ks[3], (N_A_LAYERS, 2 * A_HEAD_DIM), 0.02),
        "a_w_o": nrm(ks[4], (N_A_LAYERS, d, d), d ** -0.5 * DEEPNORM_BETA),
        "kv_w": nrm(ks[5], (d, 2 * d), d ** -0.5),
        "b_w_q": nrm(ks[6], (N_B_LAYERS, d, d), d ** -0.5),
        "b_rel_bias": nrm(ks[7], (N_B_LAYERS, B_HEADS, 2 * MAX_REL_DIST + 1), 0.2),
        "b_w_o": nrm(ks[8], (N_B_LAYERS, d, d), d ** -0.5 * DEEPNORM_BETA),
        "ln_g": 1.0 + nrm(ks[9], (DEPTH, 2, d), 0.02),
        "ln_b": nrm(ks[10], (DEPTH, 2, d), 0.02),
        "ffn_w_in": nrm(ks[11], (DEPTH, d, 2 * D_FF), d ** -0.5),
        "ffn_w_out": nrm(ks[12], (DEPTH, D_FF, d), D_FF ** -0.5 * DEEPNORM_BETA),
    }


def reference(x, a_w_qkv, a_lambda, a_subln_g, a_w_o, kv_w, b_w_q, b_rel_bias, b_w_o,
              ln_g, ln_b, ffn_w_in, ffn_w_out):
    k_band = v_band = band_valid = None
    for l in range(DEPTH):
        if l < N_A_LAYERS:
            mix = diff_attention(x, a_w_qkv[l], a_lambda[l], a_subln_g[l], a_w_o[l],
                                 lambda_init_fn(l))
        else:
            if l == N_A_LAYERS:
                k_band, v_band, band_valid = shared_kv_band(x, kv_w)
            j = l - N_A_LAYERS
            mix = chunk_attention(x, k_band, v_band, band_valid, b_w_q[j], b_rel_bias[j], b_w_o[j])
        x = layer_norm(DEEPNORM_ALPHA * x + mix, ln_g[l, 0], ln_b[l, 0])
        x = layer_norm(DEEPNORM_ALPHA * x + swiglu(x, ffn_w_in[l], ffn_w_out[l]), ln_g[l, 1], ln_b[l, 1])
    return x
```

```python
import os
import numpy as np
import concourse.bass as bass
import concourse.mybir as mybir
from concourse.bass_utils import run_bass_kernel_spmd

F32 = mybir.dt.float32
BF16 = mybir.dt.bfloat16
AF = mybir.ActivationFunctionType
ALU = mybir.AluOpType
AX = mybir.AxisListType

D = 1024
SEQ = 2048
NT = SEQ // 128
DFF = 2816
NFC = DFF // 128
ALPHA = float((2 * 2) ** 0.25)
LAMBDA_INIT = 0.8 - 0.6
LN_EPS = 1e-5
NEG = -30000.0

SAME_ENGINE_SYNC = True
LN_ADD_ENG = os.environ.get('KLNADD', 'dve')


class Op:
    __slots__ = ("eng", "fn", "deps", "sem", "milestone", "count", "dma_tag", "dma_n")

    def __init__(self, eng, fn, deps, sem):
        self.eng = eng
        self.fn = fn
        self.deps = deps
        self.sem = sem
        self.milestone = False
        self.count = 0
        self.dma_tag = None
        self.dma_n = 0


class Sched:
    ENGS = ["pe", "act", "dve", "pool", "sp"]

    def __init__(self, nc):
        self.nc = nc
        self.ops = {e: [] for e in self.ENGS}
        self.lastw = {}
        self.readers = {}
        self.epoch = 0
        self.dma_cnt = {}
        self.fences = {}

    def new_epoch(self):
        self.epoch += 1

    def snapshot(self):
        toks = set()
        for e in self.ENGS:
            for i in range(len(self.ops[e]) - 1, -1, -1):
                if self.ops[e][i].fn is not None and self.ops[e][i].dma_tag is None:
                    toks.add(("E", e, i))
                    break
        for t, n in self.dma_cnt.items():
            toks.add(("D", t, n))
        return toks

    def set_fence(self, names, toks):
        for n in names:
            self.fences[n] = toks

    def _deps(self, reads, writes):
        deps = set()
        for k in reads:
            w = self.lastw.get(k)
            if w is not None:
                deps.add(w)
        for k in writes:
            w = self.lastw.get(k)
            if w is not None:
                deps.add(w)
            for r in self.readers.get(k, ()):
                deps.add(r)
            f = self.fences.get(k[0])
            if f:
                deps |= f
        return deps

    @staticmethod
    def _reduce(deps):
        best = {}
        out = set()
        for d in deps:
            if d[0] == "E":
                if d[2] > best.get(d[1], -1):
                    best[d[1]] = d[2]
            else:
                k = ("D", d[1])
                if d[2] > best.get(k, -1):
                    best[k] = d[2]
        for k, v in best.items():
            if isinstance(k, tuple):
                out.add(("D", k[1], v))
            else:
                out.add(("E", k, v))
        return out

    def _commit(self, tok, reads, writes):
        for k in reads:
            self.readers.setdefault(k, []).append(tok)
        for k in writes:
            self.lastw[k] = tok
            self.readers[k] = []

    def add(self, eng, fn, reads=(), writes=(), extra=()):
        deps = self._reduce(self._deps(reads, writes) | set(extra))
        op = Op(eng, fn, deps, "%s%d" % (eng, self.epoch))
        idx = len(self.ops[eng])
        self.ops[eng].append(op)
        tok = ("E", eng, idx)
        self._commit(tok, reads, writes)
        return tok

    def dma(self, queue, tag, fn, reads=(), writes=(), extra=()):
        deps = self._reduce(self._deps(reads, writes) | set(extra))
        op = Op(queue, fn, deps, None)
        n = self.dma_cnt.get(tag, 0) + 1
        self.dma_cnt[tag] = n
        op.dma_tag = tag
        op.dma_n = n
        self.ops[queue].append(op)
        tok = ("D", tag, n)
        self._commit(tok, reads, writes)
        return tok

    def wait_only(self, eng, toks):
        op = Op(eng, None, set(toks), None)
        self.ops[eng].append(op)

    def finalize(self):
        nc = self.nc
        for e in self.ENGS:
            for op in self.ops[e]:
                for d in op.deps:
                    if d[0] == "E":
                        if d[1] == e and (e == "pe" or not SAME_ENGINE_SYNC):
                            continue
                        self.ops[d[1]][d[2]].milestone = True
        semnames = set()
        for e in self.ENGS:
            cnt = {}
            for op in self.ops[e]:
                if op.milestone:
                    cnt[op.sem] = cnt.get(op.sem, 0) + 1
                    op.count = cnt[op.sem]
                    semnames.add(op.sem)
        for t in self.dma_cnt:
            semnames.add("dma_" + t)
        sems = {}
        for name in sorted(semnames):
            sems[name] = nc.alloc_semaphore(name)
        self.sems = sems
        self.nsem = len(sems)
        sched = self

        def emit(ename, eng):
            waited = {}
            for op in sched.ops[ename]:
                need = {}
                for d in op.deps:
                    if d[0] == "E":
                        if d[1] == ename and (ename == "pe" or not SAME_ENGINE_SYNC):
                            continue
                        src = sched.ops[d[1]][d[2]]
                        sn, val = src.sem, src.count
                    else:
                        sn, val = "dma_" + d[1], 16 * d[2]
                    if val > need.get(sn, 0):
                        need[sn] = val
                for sn, val in need.items():
                    if waited.get(sn, 0) >= val:
                        continue
                    eng.wait_ge(sems[sn], val)
                    waited[sn] = val
                if op.fn is None:
                    continue
                inst = op.fn(eng)
                if op.dma_tag is not None:
                    inst.then_inc(sems["dma_" + op.dma_tag], 16)
                elif op.milestone:
                    inst.then_inc(sems[op.sem], 1)

        with nc.Block() as block:
            @block.tensor
            def _(e):
                emit("pe", e)

            @block.scalar
            def _(e):
                emit("act", e)

            @block.vector
            def _(e):
                emit("dve", e)

            @block.gpsimd
            def _(e):
                emit("pool", e)

            @block.sync
            def _(e):
                emit("sp", e)


def build(nseq=2, layers=(0, 1), stop_after=None):
    nc = bass.Bass("TRN2", target_bir_lowering=False)
    S = Sched(nc)

    def din(name, shape):
        return nc.dram_tensor(name, shape, F32, kind="ExternalInput").ap()

    x_d = din("x", [nseq, SEQ, D])
    wA_d = din("wA", [8, 128, 8 * 384])
    rope_d = din("rope", [2, 128, SEQ])
    lam_d = din("lam", [128, 256])
    subg_d = din("subg", [128, 128])
    woA_d = din("woA", [128, 8 * D])
    win_d = din("win", [2, NFC, 128, 8 * 256])
    wout_d = din("wout", [2, 128, NFC * D])
    wB_d = din("wB", [8, 128, 8 * 384])
    bias_d = din("biasB", [8, 128, 2 * 640])
    woB_d = din("woB", [128, 8 * D])
    lng_d = din("lng", [4, 128, D])
    lnb_d = din("lnb", [4, 128, D])
    out_d = nc.dram_tensor("out", [nseq, SEQ, D], F32, kind="ExternalOutput").ap()

    base = (nc.sbuf_base + 63) // 64 * 64
    top = nc.sbuf_top
    cur = [base]

    def dsize(dt):
        return 4 if dt == F32 else 2

    def salloc(name, shape, dt, off=None):
        nb = dsize(dt)
        for s_ in shape[1:]:
            nb *= s_
        if off is None:
            off = cur[0]
            cur[0] += (nb + 63) // 64 * 64
            assert cur[0] <= top, (name, cur[0], top)
        else:
            assert off + nb <= top, (name, off + nb, top)
        return nc.alloc_sbuf_tensor_at(name, list(shape), dt, offset=off)

    ident_bf = salloc("ident_bf", [128, 128], BF16)
    ident_f = salloc("ident_f", [128, 128], F32)
    maskT = salloc("maskT", [128, 64], BF16)
    pswap = salloc("pswap", [128, 128], BF16)
    lamrep = salloc("lamrep", [128, 256], F32)
    lamt = salloc("lamt", [128, 8], F32)
    gs = salloc("gs", [128, 128], F32)
    lng = salloc("lng", [128, D], F32)
    lnb = salloc("lnb", [128, D], F32)
    epst = salloc("epst", [128, 1], F32)
    small = salloc("small", [128, 64], F32)
    X = salloc("X", [128, NT, D], F32)
    r1 = cur[0]
    xT = salloc("xT", [128, 8, SEQ], BF16)
    r2 = cur[0]
    xTb = salloc("xTb", [128, 8, 1024], BF16, off=r1)
    NWIN = 3
    winr = [salloc("win%d" % i, [128, 8, 256], BF16, off=r1 + 16384 + i * 4096) for i in range(NWIN)]
    o = [r2]

    def r2alloc(name, shape, dt):
        nb = dsize(dt)
        for s_ in shape[1:]:
            nb *= s_
        t = salloc(name, shape, dt, off=o[0])
        o[0] += (nb + 63) // 64 * 64
        return t

    oT = r2alloc("oT", [128, 8, SEQ], BF16)
    r2b = o[0]
    QT = r2alloc("QT", [128, SEQ], BF16)
    KT = r2alloc("KT", [128, SEQ], BF16)
    V1 = r2alloc("V1", [128, NT, 130], BF16)
    NET2 = 3
    ET2 = [r2alloc("ET%d" % i, [128, 2, 512], BF16) for i in range(NET2)]
    rope_off = o[0]
    ropeC = r2alloc("ropeC", [128, SEQ], F32)
    ropeS = r2alloc("ropeS", [128, SEQ], F32)
    biasT = salloc("biasT", [128, 2, 640], BF16, off=rope_off)
    wring = [r2alloc("wring%d" % i, [128, 8, 384], BF16) for i in range(2)]
    rt1 = r2alloc("rt1", [128, 512], F32)
    rt2 = r2alloc("rt2", [128, 512], F32)
    qb16 = r2alloc("qb16", [128, 2, 512], BF16)
    nrm_on = r2alloc("nrm_on", [128, 512], BF16)
    accs = r2alloc("accs", [128, 1040], F32)
    att_end = o[0]
    wo_sb = salloc("wo_sb", [128, 8, D], BF16, off=r2b)
    hT = salloc("hT", [128, NFC, 1024], BF16, off=r2)
    wout_sb = salloc("wout_sb", [128, NFC, D], BF16, off=r2 + 49152)
    sgt = [salloc("sgt%d" % i, [128, 512], F32, off=r2 + 49152 + 45056 + i * 2048) for i in range(2)]
    ytmpA = salloc("ytmpA", [128, D], F32, off=r2 + 49152 + 45056 + 4096)
    assert r2 + 49152 + 45056 + 4096 + 4096 <= top, (r2, top)
    assert att_end <= top, (att_end - r2)
    ytmpB = salloc("ytmpB", [128, D], F32, off=r2 + 49152 + 45056)
    ytmp = [ytmpA, ytmpB]

    ps = nc.alloc_psum_tensor("ps", [128, 8 * 512], F32)

    def bank(b, n=512, off=0):
        return ps[:, b * 512 + off: b * 512 + off + n]

    S.add("pool", lambda e: e.memset(ident_f[:, :], 1.0), writes=[("c", "identf")])
    S.add("pool", lambda e: e.affine_select(out=ident_f[:, :], in_=ident_f[:, :], pattern=[[-1, 128]],
                                            compare_op=ALU.is_equal, fill=0.0, base=0, channel_multiplier=1),
          reads=[("c", "identf")], writes=[("c", "identf")])
    S.add("dve", lambda e: e.tensor_copy(out=ident_bf[:, :], in_=ident_f[:, :]), reads=[("c", "identf")],
          writes=[("c", "identb")])
    for (dlo, slo) in ((0, 32), (32, 0), (64, 96), (96, 64)):
        S.add("dve", lambda e, dlo=dlo, slo=slo: e.tensor_copy(out=pswap[:, dlo:dlo + 32], in_=ident_f[:, slo:slo + 32]),
              reads=[("c", "identf")], writes=[("c", "pswap")])
    S.add("dve", lambda e: e.memset(maskT[0:64, :], 0.0), writes=[("c", "mask0")])
    S.add("dve", lambda e: e.memset(maskT[64:128, :], NEG), writes=[("c", "mask1")])
    S.add("dve", lambda e: e.memset(epst[:, :], LN_EPS), writes=[("c", "eps")])
    CONST_KEYS = [("c", "identf"), ("c", "identb"), ("c", "mask0"), ("c", "mask1"), ("c", "eps")]
    S.dma("sp", "cst", lambda e: e.dma_start(out=lamrep[:, :], in_=lam_d[:, :]), writes=[("c", "lamrep")])
    SKIP = os.environ.get("KSKIP", "").split(",")
    if "gs" not in SKIP:
        S.dma("sp", "cst2", lambda e: e.dma_start(out=gs[:, :], in_=subg_d[:, :]), writes=[("c", "gs")])
    S.add("dve", lambda e: e.tensor_tensor(out=lamrep[:, 0:64], in0=lamrep[:, 0:64], in1=lamrep[:, 64:128],
                                           op=ALU.mult), reads=[("c", "lamrep")], writes=[("c", "lamrep")])
    S.add("dve", lambda e: e.tensor_tensor(out=lamrep[:, 128:192], in0=lamrep[:, 128:192], in1=lamrep[:, 192:256],
                                           op=ALU.mult), reads=[("c", "lamrep")], writes=[("c", "lamrep")])
    S.add("dve", lambda e: e.tensor_reduce(out=lamt[:, 0:1], in_=lamrep[:, 0:64], axis=AX.X, op=ALU.add),
          reads=[("c", "lamrep")], writes=[("c", "lamt")])
    S.add("dve", lambda e: e.tensor_reduce(out=lamt[:, 1:2], in_=lamrep[:, 128:192], axis=AX.X, op=ALU.add),
          reads=[("c", "lamrep")], writes=[("c", "lamt")])
    S.add("act", lambda e: e.activation(out=lamt[:, 2:4], in_=lamt[:, 0:2], func=AF.Exp),
          reads=[("c", "lamt")], writes=[("c", "lamt")])
    S.add("dve", lambda e: e.scalar_tensor_tensor(out=lamt[:, 4:5], in0=lamt[:, 3:4], scalar=-LAMBDA_INIT,
                                                  in1=lamt[:, 2:3], op0=ALU.add, op1=ALU.subtract),
          reads=[("c", "lamt")], writes=[("c", "neglam")])
    if "gs" not in SKIP:
        S.add("dve", lambda e: e.tensor_scalar(out=gs[:, :], in0=gs[:, :], scalar1=1.0 - LAMBDA_INIT, scalar2=None,
                                               op0=ALU.mult), reads=[("c", "gs")], writes=[("c", "gs")])

    pe_i = [0]

    def mm(out, lhsT, rhs, start, stop, reads, writes, skip=False):
        S.add("pe", lambda e: e.matmul(out, lhsT, rhs, start=start, stop=stop, skip_group_check=skip),
              reads=reads, writes=writes)

    def tr(out, in_, ident, reads, writes):
        S.add("pe", lambda e: e.transpose(out, in_, ident), reads=reads, writes=writes)

    cp_i = [0]

    def evac(out, in_, reads, writes, scale=None, eng=None):
        cp_i[0] += 1
        writes = list(writes) + [k for k in reads if k[0] == "ps"]
        reads = [k for k in reads if k[0] != "ps"]
        if scale is not None:
            S.add("act", lambda e: e.mul(out, in_, scale), reads=reads, writes=writes)
        elif eng == "act" or (eng is None and cp_i[0] % 2):
            S.add("act", lambda e: e.copy(out=out, in_=in_), reads=reads, writes=writes)
        else:
            S.add("dve", lambda e: e.tensor_copy(out=out, in_=in_), reads=reads, writes=writes)

    def load_x(s):
        if "loadx" in SKIP:
            return
        extra0 = [("D", "ost", S.dma_cnt["ost"])] if S.dma_cnt.get("ost") else []
        for t in range(NT):
            extra = list(extra0)
            if S.dma_cnt.get("ost%d" % t):
                extra.append(("D", "ost%d" % t, S.dma_cnt["ost%d" % t]))
            S.dma("sp", "x%d" % t, lambda e, t=t: e.dma_start(out=X[:, t, :], in_=x_d[s, t * 128:(t + 1) * 128, :]),
                  writes=[("X", t)], extra=extra)

    tb = [0]

    def transposes(tiles, dst, dkey, toff):
        for t in tiles:
            for half in range(2):
                b = tb[0] % 4
                tb[0] += 1
                for c4 in range(4):
                    c = half * 4 + c4
                    tr(bank(b, 128, c4 * 128), X[:, t, c * 128:(c + 1) * 128], ident_f[:, :],
                       reads=[("X", t), ("c", "identf")], writes=[("ps", b)])
                tl = t - toff
                evac(dst[:, half * 4:half * 4 + 4, tl * 128:(tl + 1) * 128],
                     bank(b).rearrange("p (a b) -> p a b", a=4),
                     reads=[("ps", b)], writes=[(dkey, half, t)])

    def xT_keys(dkey, t0, t1):
        return [(dkey, h, t) for h in range(2) for t in range(t0, t1)]

    def ln_parts(t, psrc, psrc_keys, yi, final, s):
        p = yi % 2
        y = ytmp[p]
        yk = ("y", p)
        c0 = 0 if p == 0 else 44

        def sc(a, b):
            return small[:, c0 + a:c0 + b]

        def A():
            S.add("dve", lambda e: e.scalar_tensor_tensor(out=y[:, :], in0=X[:, t, :], scalar=ALPHA, in1=psrc,
                                                          op0=ALU.mult, op1=ALU.add),
                  reads=[("X", t)], writes=[yk] + psrc_keys)
            S.add("dve", lambda e: e.bn_stats(out=sc(0, 6), in_=y[:, 0:512]), reads=[yk], writes=[("st", p, 0)])
            S.add("dve", lambda e: e.bn_stats(out=sc(6, 12), in_=y[:, 512:1024]), reads=[yk],
                  writes=[("st", p, 1)])
            S.add("dve", lambda e: e.bn_aggr(out=sc(12, 14), in_=sc(0, 12)), reads=[("st", p, 0), ("st", p, 1)],
                  writes=[("st", p, 2)])
            S.add("act", lambda e: e.activation(out=sc(14, 15), in_=sc(13, 14), func=AF.Ln,
                                                bias=epst[:, 0:1], scale=1.0),
                  reads=[("st", p, 2), ("c", "eps")], writes=[("st", p, 3)])
            S.add("act", lambda e: e.activation(out=sc(15, 16), in_=sc(14, 15), func=AF.Exp, scale=-0.5),
                  reads=[("st", p, 3)], writes=[("st", p, 4)])

        def B1():
            S.add("dve", lambda e: e.tensor_scalar(out=sc(16, 17), in0=sc(12, 13), scalar1=sc(15, 16),
                                                   scalar2=-1.0, op0=ALU.mult, op1=ALU.mult),
                  reads=[("st", p, 2), ("st", p, 4)], writes=[("st", p, 5)])
            S.add("act", lambda e: e.activation(out=y[:, :], in_=y[:, :], func=AF.Identity,
                                                bias=sc(16, 17), scale=sc(15, 16)),
                  reads=[yk, ("st", p, 4), ("st", p, 5)], writes=[yk])

        def B2():
            S.add("dve", lambda e: e.tensor_tensor(out=y[:, :], in0=y[:, :], in1=lng[:, :], op=ALU.mult),
                  reads=[yk, ("ln", "g")], writes=[yk])
            S.add(LN_ADD_ENG, lambda e: e.tensor_tensor(out=X[:, t, :], in0=y[:, :], in1=lnb[:, :], op=ALU.add),
                  reads=[yk, ("ln", "b")], writes=[("X", t)])
            if final:
                S.dma("sp", "ost%d" % t, lambda e: e.dma_start(out=out_d[s, t * 128:(t + 1) * 128, :], in_=X[:, t, :]),
                      reads=[("X", t)])

        return A, B1, B2

    def layer_norm(t, psrc, psrc_keys, li, yi, final, s):
        A, B1, B2 = ln_parts(t, psrc, psrc_keys, yi, final, s)
        A()
        B1()
        B2()

    def load_ln(li):
        S.dma("sp", "lng", lambda e: e.dma_start(out=lng[:, :], in_=lng_d[li]), writes=[("ln", "g")])
        S.dma("sp", "lnb", lambda e: e.dma_start(out=lnb[:, :], in_=lnb_d[li]), writes=[("ln", "b")])

    wr_i = [0]
    et_i = [0]
    sb_i = [0]

    def attention(s, layer):
        isA = layer == 0
        W = 129 if isA else 65
        per_bank = 512 // W

        def acc(m, q4):
            if isA:
                if q4 < 3:
                    b, col = 4 + m, q4 * W
                else:
                    b, col = 6, m * W
            else:
                b, col = 4 + m, q4 * W
            return bank(b, W, col), None, b

        accs_t = accs.tensor if hasattr(accs, "tensor") else accs
        pstep = accs[:, :].ap[0][0]

        def accs_ap(off, dims):
            return bass.AP(accs_t, off, [[pstep, 128]] + [list(d_) for d_ in dims])

        started = set()
        pending = []
        NDEFER = int(os.environ.get('KNDEFER', '8')) if isA else 4
        NS2 = int(os.environ.get('KNS2', '6'))

        if isA:
            S.dma("sp", "rope", lambda e: e.dma_start(out=ropeC[:, :], in_=rope_d[0]), writes=[("rope", 0)])
            S.dma("sp", "rope", lambda e: e.dma_start(out=ropeS[:, :], in_=rope_d[1]), writes=[("rope", 1)])
            S.lastw[("rope", 0)] = S.lastw[("rope", 1)]
        if isA:
            S.add("dve", lambda e: e.memset(V1[:, :, 128:129], 1.0), writes=[("V1ones",)])
        else:
            S.add("dve", lambda e: e.memset(V1[:, :, 64:65], 1.0), writes=[("V1ones",)])
            S.add("dve", lambda e: e.memset(V1[:, :, 129:130], 1.0), reads=[("V1ones",)], writes=[("V1ones",)])
        ncol = 384
        wsrc = wA_d if isA else wB_d
        for u in range(int(os.environ.get('KUNITS', '8'))):
            slot = wr_i[0] % 2
            wr_i[0] += 1
            wk = ("wr", slot)
            wt = wring[slot]
            S.dma("pool", "wr%d" % slot,
                  lambda e, u=u, wt=wt: e.dma_start(out=wt[:, :, 0:ncol],
                                                    in_=wsrc[u].rearrange("p (a b) -> p a b", a=8)),
                  writes=[wk])
            if not isA:
                S.dma("pool", "bias", lambda e, u=u: e.dma_start(out=biasT[:, :, :],
                                                                 in_=bias_d[u].rearrange("p (a b) -> p a b", a=2)),
                      writes=[("biasT",)])
                S.add("dve", lambda e: e.memset(biasT[64:128, :, 0:64], NEG), reads=[("biasT",)], writes=[("biasT",)])
                S.add("dve", lambda e: e.memset(biasT[0:64, :, 576:640], NEG), reads=[("biasT",)], writes=[("biasT",)])
            for p_ in [p_ for p_ in pending if p_[2] == "s2"]:
                p_[1]()
                pending.remove(p_)
            vc = 256
            for t4 in range(4):
                vb = 7 - t4
                for tt in range(4):
                    t = t4 * 4 + tt
                    for k in range(8):
                        mm(bank(vb, 128, tt * 128), xT[:, k, t * 128:(t + 1) * 128], wt[:, k, vc:vc + 128],
                           k == 0, k == 7, reads=[wk] + ([("xT", 0, t), ("xT", 1, t)] if k == 0 else []),
                           writes=[("ps", vb)])
                vkeys = [("V1", t4 * 4 + tt) for tt in range(4)]
                if isA:
                    evac(V1[:, t4 * 4:t4 * 4 + 4, 0:128], bank(vb).rearrange("p (a b) -> p a b", a=4),
                         reads=[("ps", vb), ("V1ones",)], writes=vkeys)
                else:
                    for m in range(2):
                        evac(V1[:, t4 * 4:t4 * 4 + 4, m * 65:m * 65 + 64],
                             bank(vb).rearrange("p (a b) -> p a b", a=4)[:, :, m * 64:(m + 1) * 64],
                             reads=[("ps", vb), ("V1ones",)], writes=vkeys)
            for j in range(4):
                tsl = slice(j * 512, (j + 1) * 512)
                xk = xT_keys("xT", j * 4, j * 4 + 4)
                pbase = (j % 2) * 4 if isA else (j % 2) * 2
                for pi in range(2):
                    for k in range(8):
                        mm(bank(pbase + pi), wt[:, k, pi * 128:(pi + 1) * 128], xT[:, k, tsl], k == 0, k == 7,
                           reads=[wk] + (xk if k == 0 else []), writes=[("ps", pbase + pi)])
                if isA:
                    for pi in range(2):
                        b = pbase + pi
                        S.add("act", lambda e, pi=pi, b=b: e.copy(out=qb16[:, pi, :], in_=bank(b)),
                              reads=[], writes=[("qb16", pi), ("ps", b)])
                    for pi in range(2):
                        mm(bank(pbase + 2 + pi), pswap[:, :], qb16[:, pi, :], True, True,
                           reads=[("qb16", pi), ("c", "pswap")], writes=[("ps", pbase + 2 + pi)])
                    for (pi, dst, dk) in ((0, QT, "QT"), (1, KT, "KT")):
                        b = pbase + pi
                        S.add("dve", lambda e, b=b, tsl=tsl: e.tensor_tensor(out=rt1[:, :], in0=bank(b),
                                                                             in1=ropeC[:, tsl], op=ALU.mult),
                              reads=[("rope", 0)], writes=[("rt", 1), ("ps", b)])
                        S.add("dve", lambda e, b=b, tsl=tsl: e.tensor_tensor(out=rt2[:, :], in0=bank(b + 2),
                                                                             in1=ropeS[:, tsl], op=ALU.mult),
                              reads=[("rope", 0)], writes=[("rt", 2), ("ps", b + 2)])
                        S.add("dve", lambda e, dst=dst, tsl=tsl: e.tensor_tensor(
                            out=dst[:, tsl], in0=rt1[:, :], in1=rt2[:, :], op=ALU.add),
                            reads=[("rt", 1), ("rt", 2)], writes=[(dk, j)])
                else:
                    evac(QT[:, tsl], bank(pbase), reads=[("ps", pbase)], writes=[("QT", j)], scale=0.125)
                    evac(KT[:, tsl], bank(pbase + 1), reads=[("ps", pbase + 1)], writes=[("KT", j)])
            steps = []
            for qb in range(4):
                kts = list(range(0, 4 * qb + 4)) if isA else list(range(max(0, 4 * qb - 4), 4 * qb + 4))
                for kt in kts:
                    steps.append((qb, kt, kt == kts[-1]))

            def emit_scores(qb, kt):
                qlo = max(512 * qb, 128 * kt)
                qhi = 512 * (qb + 1) if isA else min(512 * (qb + 1), 128 * kt + 640)
                n = qhi - qlo
                ksl = slice(kt * 128, (kt + 1) * 128)
                qkeys = [("QT", qb)]
                kkeys = [("KT", kt // 4)]
                ets = []
                b0 = sb_i[0] % 4
                ei = et_i[0] % NET2
                et_i[0] += 1
                et2 = ET2[ei]
                ek = ("ET", ei)
                for m in range(2):
                    b = sb_i[0] % 4
                    sb_i[0] += 1
                    psl = slice(m * 64, (m + 1) * 64)
                    pk = ("ps", b)
                    if isA:
                        if qlo == 128 * kt:
                            mm(bank(b, 64), ident_bf[:, :], maskT[:, :], True, False,
                               reads=CONST_KEYS, writes=[pk])
                        else:
                            mm(bank(b, n), KT[psl, ksl], QT[psl, qlo:qhi], True, True,
                               reads=qkeys + kkeys, writes=[pk])
                    else:
                        mm(bank(b, n), ident_bf[:, :], biasT[:, m, qlo - 128 * kt:qhi - 128 * kt], True, False,
                           reads=CONST_KEYS + [("biasT",)], writes=[pk])
                    ets.append((et2[:, m, :], ek))
                if isA and qlo == 128 * kt:
                    for m in range(2):
                        psl = slice(m * 64, (m + 1) * 64)
                        mm(bank(b0 + m, 64), KT[psl, ksl], QT[psl, qlo:qlo + 64], False, True,
                           reads=qkeys + kkeys, writes=[("ps", b0 + m)])
                    for m in range(2):
                        psl = slice(m * 64, (m + 1) * 64)
                        mm(bank(b0 + m, n - 64, 64), KT[psl, ksl], QT[psl, qlo + 64:qhi], True, True,
                           reads=[], writes=[("ps", b0 + m)])
                if not isA:
                    for m in range(2):
                        psl = slice(m * 64, (m + 1) * 64)
                        mm(bank(b0 + m, n), KT[psl, ksl], QT[psl, qlo:qhi], False, True,
                           reads=qkeys + kkeys, writes=[("ps", b0 + m)])
                S.add("act", lambda e, et2=et2, b0=b0, n=n: e.activation(
                    out=et2[:, :, 0:n], in_=ps[:, b0 * 512:(b0 + 2) * 512].rearrange("p (a b) -> p a b", a=2)[:, :, 0:n],
                    func=AF.Exp, scale=(0.125 if isA else 1.0)),
                    reads=[], writes=[ek, ("ps", b0), ("ps", b0 + 1)])
                return (qlo, qhi, ets)

            def emit_pv(qb, kt, info):
                qlo, qhi, ets = info
                for m in range(2):
                    et, ek = ets[m]
                    for qt in range(qlo // 128, qhi // 128):
                        last = kt == qt
                        a_ap, a_k, a_b = acc(m, qt % 4)
                        first = (u, qb, a_b) not in started
                        started.add((u, qb, a_b))
                        if isA:
                            rhs = V1[:, kt, 0:129]
                        else:
                            rhs = V1[:, kt, m * 65:m * 65 + 65]
                        off = qt * 128 - qlo
                        mm(a_ap, et[:, off:off + 128], rhs, first, last,
                           reads=[ek, ("V1", kt)], writes=[("ps", a_b)], skip=True)

            def emit_norm(qb):
                AK = [("accs", 0)]
                if isA:
                    S.add("dve", lambda e: e.tensor_copy(out=accs[:, 0:3 * W], in_=bank(4, 3 * W)),
                          writes=AK + [("ps", 4)])
                    S.add("dve", lambda e: e.tensor_copy(out=accs[:, 4 * W:7 * W], in_=bank(5, 3 * W)),
                          writes=AK + [("ps", 5)])
                    S.add("dve", lambda e: e.tensor_copy(out=accs_ap(3 * W, [[4 * W, 2], [1, W]]),
                                                        in_=bank(6, 2 * W).rearrange("p (a b) -> p a b", a=2)),
                          writes=AK + [("ps", 6)])
                else:
                    S.add("dve", lambda e: e.tensor_copy(out=accs[:, 0:4 * W], in_=bank(4, 4 * W)),
                          writes=AK + [("ps", 4)])
                    S.add("dve", lambda e: e.tensor_copy(out=accs[:, 4 * W:8 * W], in_=bank(5, 4 * W)),
                          writes=AK + [("ps", 5)])
                Wv = W - 1
                S.add("dve", lambda e: e.reciprocal(out=small[:, 20:28], in_=accs_ap(Wv, [[W, 8]])),
                      reads=AK, writes=[("nr", 0)])

                def rbc(m):
                    return small[:, 20 + 4 * m:24 + 4 * m].unsqueeze(2).broadcast_to([128, 4, Wv])

                def av(m):
                    return accs_ap(m * 4 * W, [[W, 4], [1, Wv]])

                onk = ("on",)
                if isA:
                    t4 = rt1[:, :].rearrange("p (a b) -> p a b", a=4)
                    u4 = rt2[:, :].rearrange("p (a b) -> p a b", a=4)
                    S.add("dve", lambda e: e.tensor_tensor(out=t4, in0=av(1), in1=rbc(1), op=ALU.mult),
                          reads=AK + [("nr", 0)], writes=[("rt", 1)])
                    S.add("dve", lambda e: e.tensor_tensor(out=u4, in0=av(0), in1=rbc(0), op=ALU.mult),
                          reads=AK + [("nr", 0)], writes=[("rt", 2)])
                    S.add("dve", lambda e: e.scalar_tensor_tensor(out=rt1[:, :], in0=rt1[:, :], scalar=lamt[:, 4:5],
                                                                  in1=rt2[:, :], op0=ALU.mult, op1=ALU.add),
                          reads=[("rt", 2), ("c", "neglam")], writes=[("rt", 1)])
                    S.add("dve", lambda e: e.tensor_tensor(out=rt2[:, :], in0=rt1[:, :], in1=rt1[:, :], op=ALU.mult),
                          reads=[("rt", 1)], writes=[("rt", 2)])
                    S.add("dve", lambda e: e.tensor_reduce(out=small[:, 28:32], in_=u4, axis=AX.X, op=ALU.add),
                          reads=[("rt", 2)], writes=[("nr", 2)])
                    S.add("dve", lambda e: e.tensor_scalar(out=small[:, 32:36], in0=small[:, 28:32],
                                                           scalar1=1.0 / 128.0, scalar2=LN_EPS,
                                                           op0=ALU.mult, op1=ALU.add),
                          reads=[("nr", 2)], writes=[("nr", 3)])
                else:
                    on4 = nrm_on[:, :].rearrange("p (a b) -> p a b", a=4)
                    for m in range(2):
                        S.add("dve", lambda e, m=m: e.tensor_tensor(out=on4[:, :, m * 64:(m + 1) * 64], in0=av(m),
                                                                    in1=rbc(m), op=ALU.mult),
                              reads=AK + [("nr", 0)], writes=[onk])

            def emit_norm2(qb):
                t4 = rt1[:, :].rearrange("p (a b) -> p a b", a=4)
                onk = ("on",)
                S.add("act", lambda e: e.activation(out=small[:, 36:40], in_=small[:, 32:36], func=AF.Ln),
                      reads=[("nr", 3)], writes=[("nr", 4)])
                S.add("act", lambda e: e.activation(out=small[:, 40:44], in_=small[:, 36:40], func=AF.Exp,
                                                    scale=-0.5), reads=[("nr", 4)], writes=[("nr", 5)])
                S.add("dve", lambda e: e.tensor_tensor(
                    out=t4, in0=t4, in1=small[:, 40:44].unsqueeze(2).broadcast_to([128, 4, 128]), op=ALU.mult),
                    reads=[("nr", 5)], writes=[("rt", 1)])
                S.add("dve", lambda e: e.tensor_tensor(
                    out=nrm_on[:, :].rearrange("p (a b) -> p a b", a=4), in0=t4,
                    in1=gs[:, :].unsqueeze(1).broadcast_to([128, 4, 128]), op=ALU.mult),
                    reads=[("rt", 1), ("c", "gs")], writes=[onk])

            def emit_norm_pe(qb, u):
                for q4 in range(4):
                    tr(bank(7, 64, q4 * 64).bitcast(BF16), nrm_on[:, q4 * 128:(q4 + 1) * 128], ident_bf[:, :],
                       reads=[("on",), ("c", "identb")], writes=[("ps", 7)])
                evac(oT[:, u, qb * 512:(qb + 1) * 512], bank(7, 256).bitcast(BF16),
                     reads=[("ps", 7)], writes=[("oT", u, qb)])

            infos = {}
            if os.environ.get('KSTEPS'):
                steps = steps[:int(os.environ['KSTEPS'])]
            for i in range(len(steps) + 1):
                if i < len(steps):
                    infos[i] = emit_scores(steps[i][0], steps[i][1])
                if i >= 1:
                    qb, kt, lastk = steps[i - 1]
                    emit_pv(qb, kt, infos.pop(i - 1))
                    for p_ in list(pending):
                        p_[0] -= 1
                        if p_[0] <= 0:
                            p_[1]()
                            pending.remove(p_)
                    if lastk and not os.environ.get('KNONORM'):
                        for p_ in pending:
                            p_[1]()
                        del pending[:]
                        emit_norm(qb)
                        if isA:
                            pending.append([NS2, (lambda qb=qb, f=emit_norm2: f(qb)), "s2"])
                        pending.append([NDEFER, (lambda qb=qb, u=u, f=emit_norm_pe: f(qb, u)), "pe"])

        for p_ in pending:
            p_[1]()
        del pending[:]

    def out_proj(s, layer):
        wsrc = woA_d if layer == 0 else woB_d
        S.dma("pool", "wo", lambda e: e.dma_start(out=wo_sb[:, :, :], in_=wsrc.rearrange("p (a b) -> p a b", a=8)),
              writes=[("wo",)])
        load_ln(layer * 2)
        prev = None
        for t in range(NT):
            pb = (t % 2) * 2
            for fo in range(2):
                for c in range(8):
                    mm(bank(pb + fo), oT[:, c, t * 128:(t + 1) * 128], wo_sb[:, c, fo * 512:(fo + 1) * 512],
                       c == 0, c == 7, reads=[("wo",), ("oT", c, t // 4)], writes=[("ps", pb + fo)])
            A, B1, B2 = ln_parts(t, ps[:, pb * 512:(pb + 2) * 512], [("ps", pb), ("ps", pb + 1)], t, False, s)
            if prev is not None:
                prev[0]()
            A()
            if prev is not None:
                prev[1]()
            prev = (B1, B2)
        prev[0]()
        prev[1]()

    wi_i = [0]
    fb_i = [0]

    def ffn(s, layer, final):
        toks = []
        for q in range(4):
            c0 = q * 6
            c1 = min(NFC, c0 + 6)
            toks.append(S.dma("pool", "wout%d" % q,
                              lambda e, c0=c0, c1=c1: e.dma_start(
                                  out=wout_sb[:, c0:c1, :],
                                  in_=wout_d[layer][:, c0 * D:c1 * D].rearrange("p (a b) -> p a b", b=D)),
                              writes=[("wout", c) for c in range(c0, c1)]))
        load_ln(layer * 2 + 1)
        for blk in range(2):
            t0 = blk * 8
            transposes(range(t0, t0 + 8), xTb, "xTb", t0)
            for c in range(NFC):
                slot = wi_i[0] % NWIN
                wi_i[0] += 1
                wt = winr[slot]
                wk = ("win", slot)
                S.dma("pool", "win%d" % slot,
                      lambda e, c=c, wt=wt: e.dma_start(out=wt[:, :, :],
                                                        in_=win_d[layer, c].rearrange("p (a b) -> p a b", a=8)),
                      writes=[wk])
                for half in range(2):
                    pb = (fb_i[0] % 2) * 2
                    fb_i[0] += 1
                    tsl = slice(half * 512, (half + 1) * 512)
                    xk = xT_keys("xTb", t0 + half * 4, t0 + half * 4 + 4)
                    for gu in range(2):
                        for k in range(8):
                            mm(bank(pb + gu), wt[:, k, gu * 128:(gu + 1) * 128], xTb[:, k, tsl], k == 0, k == 7,
                               reads=[wk] + (xk if k == 0 else []), writes=[("ps", pb + gu)])
                    sg = sgt[half]
                    S.add("act", lambda e, sg=sg, pb=pb: e.activation(out=sg[:, :], in_=bank(pb), func=AF.Silu),
                          reads=[], writes=[("sg", half), ("ps", pb)])
                    S.add("dve", lambda e, sg=sg, pb=pb, c=c, tsl=tsl: e.tensor_tensor(out=hT[:, c, tsl], in0=sg[:, :],
                                                                              in1=bank(pb + 1), op=ALU.mult),
                          reads=[("sg", half)], writes=[("hT", c, half), ("ps", pb + 1)])
            for tl in range(8):
                t = t0 + tl
                pb = 4 + (tl % 2) * 2
                for fo in range(2):
                    for c in range(NFC):
                        mm(bank(pb + fo), hT[:, c, tl * 128:(tl + 1) * 128], wout_sb[:, c, fo * 512:(fo + 1) * 512],
                           c == 0, c == NFC - 1, reads=[("wout", c), ("hT", c, tl // 4)], writes=[("ps", pb + fo)])
                layer_norm(t, ps[:, pb * 512:(pb + 2) * 512], [("ps", pb), ("ps", pb + 1)], layer * 2 + 1, 0,
                           final, s)

    def dump_oT(s):
        kd = os.environ.get('KDUMP', 'oT')
        if kd == 'oT':
            S.dma("pool", "ost", lambda e: e.dma_start(
                out=out_d[s].rearrange("(p a) d -> p (a d)", p=128),
                in_=oT[:, :, :].rearrange("p a b -> p (a b)")),
                reads=[("oT", u, qb) for u in range(8) for qb in range(4)])
        elif kd == 'QK':
            S.dma("pool", "ost", lambda e: e.dma_start(
                out=out_d[s].rearrange("(p a) d -> p (a d)", p=128)[:, 0:2048], in_=QT[:, :]),
                reads=[("QT", j) for j in range(4)])
            S.dma("pool", "ost", lambda e: e.dma_start(
                out=out_d[s].rearrange("(p a) d -> p (a d)", p=128)[:, 2048:4096], in_=KT[:, :]),
                reads=[("KT", j) for j in range(4)])
            S.dma("pool", "ost", lambda e: e.dma_start(
                out=out_d[s].rearrange("(p a) d -> p (a d)", p=128)[:, 4096:4096 + 16 * 130],
                in_=V1[:, :, :].rearrange("p a b -> p (a b)")),
                reads=[("V1", j) for j in range(16)])

    def dump_x(s):
        for t in range(NT):
            S.dma("sp", "ost", lambda e, t=t: e.dma_start(out=out_d[s, t * 128:(t + 1) * 128, :], in_=X[:, t, :]),
                  reads=[("X", t)])

    last_store = None
    for s in range(nseq):
        S.new_epoch()
        load_x(s)
        done = False
        for layer in layers:
            if stop_after == ("init",):
                break
            S.new_epoch()
            transposes(range(NT), xT, "xT", 0)
            if stop_after == ("xT",):
                S.dma("pool", "ost", lambda e: e.dma_start(
                    out=out_d[s].rearrange("(p a) d -> p (a d)", p=128),
                    in_=xT[:, :, :].rearrange("p a b -> p (a b)")),
                    reads=xT_keys("xT", 0, NT))
                break
            attention(s, layer)
            S.set_fence(["wo", "y", "wout", "xTb", "win", "sg"], S.snapshot())
            if stop_after == ("attn", layer):
                done = True
                break
            S.new_epoch()
            out_proj(s, layer)
            S.set_fence(["hT", "sg"], S.snapshot())
            if stop_after == ("op", layer):
                done = True
                break
            S.new_epoch()
            final = (layer == layers[-1]) and stop_after is None
            ffn(s, layer, final)
            S.set_fence(["xT", "wr", "QT", "KT", "V1", "V1ones", "ET", "rope", "biasT", "rt",                          "on", "oT", "accs", "qb16"], S.snapshot())
            if stop_after == ("ffn", layer):
                done = True
                break
        if stop_after == ("xT",):
            pass
        elif stop_after is not None:
            if stop_after[0] == "attn":
                dump_oT(s)
            else:
                dump_x(s)
    S.wait_only("sp", [("D", tag, cnt) for tag, cnt in S.dma_cnt.items() if tag.startswith("ost")])
    S.finalize()
    return nc, S


def prep_inputs(x, a_w_qkv, a_lambda, a_subln_g, a_w_o, kv_w, b_w_q, b_rel_bias, b_w_o,
                ln_g, ln_b, ffn_w_in, ffn_w_out):
    f = np.float32
    wqkv = np.asarray(a_w_qkv, f)[0]
    wA = np.empty((8, 128, 8, 384), f)
    for h in range(8):
        qc = np.arange(h * 128, (h + 1) * 128)
        cols = np.concatenate([qc, 1024 + qc, 2048 + qc])
        blk = wqkv[:, cols]
        wA[h] = blk.reshape(8, 128, 384).transpose(1, 0, 2)
    wA = np.ascontiguousarray(wA.reshape(8, 128, 8 * 384))
    inv = 10000.0 ** (-np.arange(0, 64, 2, dtype=np.float64) / 64.0)
    ang = np.arange(SEQ, dtype=np.float64)[:, None] * inv[None, :]
    cos = np.cos(ang).astype(f).T
    sin = np.sin(ang).astype(f).T
    C = np.concatenate([cos, cos, cos, cos], axis=0)
    Sg = np.concatenate([-sin, sin, -sin, sin], axis=0)
    rope = np.ascontiguousarray(np.stack([C, Sg]).astype(f))
    lam = np.ascontiguousarray(np.broadcast_to(np.asarray(a_lambda, f)[0].reshape(1, 256), (128, 256)))
    subg = np.ascontiguousarray(np.broadcast_to(np.asarray(a_subln_g, f)[0].reshape(1, 128), (128, 128)))

    def kmajor(w, nk):
        n = w.shape[1]
        return np.ascontiguousarray(w.reshape(nk, 128, n).transpose(1, 0, 2).reshape(128, nk * n))

    woA = kmajor(np.asarray(a_w_o, f)[0], 8)
    woB = kmajor(np.asarray(b_w_o, f)[0], 8)
    win = np.empty((2, NFC, 128, 8 * 256), f)
    w_in = np.asarray(ffn_w_in, f)
    for l in range(2):
        for c in range(NFC):
            cols = np.concatenate([np.arange(c * 128, (c + 1) * 128), DFF + np.arange(c * 128, (c + 1) * 128)])
            win[l, c] = kmajor(w_in[l][:, cols], 8)
    wout = np.stack([kmajor(np.asarray(ffn_w_out, f)[l], NFC) for l in range(2)])
    kvw = np.asarray(kv_w, f)
    wq = np.asarray(b_w_q, f)[0]
    wB = np.empty((8, 128, 8 * 384), f)
    for p in range(8):
        pc = np.arange(p * 128, (p + 1) * 128)
        blk = np.concatenate([wq[:, pc], kvw[:, pc], kvw[:, 1024 + pc]], axis=1)
        wB[p] = kmajor(blk, 8)
    tab = np.asarray(b_rel_bias, f)[0]
    i = np.arange(128)[:, None]
    j = np.arange(640)[None, :]
    idx = np.clip(i - j, -256, 256) + 256
    bt = tab[:, idx]
    biasB = np.ascontiguousarray(bt.reshape(8, 2, 128, 640).transpose(0, 2, 1, 3).reshape(8, 128, 1280))
    g = np.asarray(ln_g, f).reshape(4, 1, D)
    b = np.asarray(ln_b, f).reshape(4, 1, D)
    lng = np.ascontiguousarray(np.broadcast_to(g, (4, 128, D)))
    lnb = np.ascontiguousarray(np.broadcast_to(b, (4, 128, D)))
    shared = dict(wA=wA, rope=rope, lam=lam, subg=subg, woA=woA, win=win, wout=np.ascontiguousarray(wout),
                  wB=wB, biasB=biasB, woB=woB, lng=lng, lnb=lnb)
    return shared


_CACHE = {}


def kernel(x, a_w_qkv, a_lambda, a_subln_g, a_w_o, kv_w, b_w_q, b_rel_bias, b_w_o,
           ln_g, ln_b, ffn_w_in, ffn_w_out):
    shared = prep_inputs(x, a_w_qkv, a_lambda, a_subln_g, a_w_o, kv_w, b_w_q, b_rel_bias, b_w_o,
                         ln_g, ln_b, ffn_w_in, ffn_w_out)
    x = np.asarray(x, np.float32)
    if "nc" not in _CACHE:
        _CACHE["nc"] = build(nseq=2)[0]
    nc = _CACHE["nc"]
    in_maps = []
    for c in range(8):
        m = dict(shared)
        m["x"] = np.ascontiguousarray(x[2 * c:2 * c + 2])
        in_maps.append(m)
    res = run_bass_kernel_spmd(nc, in_maps, core_ids=list(range(8)))
    out = np.concatenate([np.asarray(r["out"], np.float32) for r in res.results], axis=0)
    return out.reshape(16, SEQ, D)
```

```python
import os
import numpy as np
import concourse.bass as bass
import concourse.mybir as mybir
from concourse.bass_utils import run_bass_kernel_spmd

F32 = mybir.dt.float32
BF16 = mybir.dt.bfloat16
AF = mybir.ActivationFunctionType
ALU = mybir.AluOpType
AX = mybir.AxisListType

D = 1024
SEQ = 2048
NT = SEQ // 128
DFF = 2816
NFC = DFF // 128
ALPHA = float((2 * 2) ** 0.25)
LAMBDA_INIT = 0.8 - 0.6
LN_EPS = 1e-5
NEG = -30000.0

SAME_ENGINE_SYNC = True
LN_ADD_ENG = os.environ.get('KLNADD', 'dve')


class Op:
    __slots__ = ("eng", "fn", "deps", "sem", "milestone", "count", "dma_tag", "dma_n")

    def __init__(self, eng, fn, deps, sem):
        self.eng = eng
        self.fn = fn
        self.deps = deps
        self.sem = sem
        self.milestone = False
        self.count = 0
        self.dma_tag = None
        self.dma_n = 0


class Sched:
    ENGS = ["pe", "act", "dve", "pool", "sp"]

    def __init__(self, nc):
        self.nc = nc
        self.ops = {e: [] for e in self.ENGS}
        self.lastw = {}
        self.readers = {}
        self.epoch = 0
        self.dma_cnt = {}
        self.fences = {}

    def new_epoch(self):
        self.epoch += 1

    def snapshot(self):
        toks = set()
        for e in self.ENGS:
            for i in range(len(self.ops[e]) - 1, -1, -1):
                if self.ops[e][i].fn is not None and self.ops[e][i].dma_tag is None:
                    toks.add(("E", e, i))
                    break
        for t, n in self.dma_cnt.items():
            toks.add(("D", t, n))
        return toks

    def set_fence(self, names, toks):
        for n in names:
            self.fences[n] = toks

    def _deps(self, reads, writes):
        deps = set()
        for k in reads:
            w = self.lastw.get(k)
            if w is not None:
                deps.add(w)
        for k in writes:
            w = self.lastw.get(k)
            if w is not None:
                deps.add(w)
            for r in self.readers.get(k, ()):
                deps.add(r)
            f = self.fences.get(k[0])
            if f:
                deps |= f
        return deps

    @staticmethod
    def _reduce(deps):
        best = {}
        out = set()
        for d in deps:
            if d[0] == "E":
                if d[2] > best.get(d[1], -1):
                    best[d[1]] = d[2]
            else:
                k = ("D", d[1])
                if d[2] > best.get(k, -1):
                    best[k] = d[2]
        for k, v in best.items():
            if isinstance(k, tuple):
                out.add(("D", k[1], v))
            else:
                out.add(("E", k, v))
        return out

    def _commit(self, tok, reads, writes):
        for k in reads:
            self.readers.setdefault(k, []).append(tok)
        for k in writes:
            self.lastw[k] = tok
            self.readers[k] = []

    def add(self, eng, fn, reads=(), writes=(), extra=()):
        deps = self._reduce(self._deps(reads, writes) | set(extra))
        op = Op(eng, fn, deps, "%s%d" % (eng, self.epoch))
        idx = len(self.ops[eng])
        self.ops[eng].append(op)
        tok = ("E", eng, idx)
        self._commit(tok, reads, writes)
        return tok

    def dma(self, queue, tag, fn, reads=(), writes=(), extra=()):
        deps = self._reduce(self._deps(reads, writes) | set(extra))
        op = Op(queue, fn, deps, None)
        n = self.dma_cnt.get(tag, 0) + 1
        self.dma_cnt[tag] = n
        op.dma_tag = tag
        op.dma_n = n
        self.ops[queue].append(op)
        tok = ("D", tag, n)
        self._commit(tok, reads, writes)
        return tok

    def wait_only(self, eng, toks):
        op = Op(eng, None, set(toks), None)
        self.ops[eng].append(op)

    def finalize(self):
        nc = self.nc
        for e in self.ENGS:
            for op in self.ops[e]:
                for d in op.deps:
                    if d[0] == "E":
                        if d[1] == e and (e == "pe" or not SAME_ENGINE_SYNC):
                            continue
                        self.ops[d[1]][d[2]].milestone = True
        semnames = set()
        for e in self.ENGS:
            cnt = {}
            for op in self.ops[e]:
                if op.milestone:
                    cnt[op.sem] = cnt.get(op.sem, 0) + 1
                    op.count = cnt[op.sem]
                    semnames.add(op.sem)
        for t in self.dma_cnt:
            semnames.add("dma_" + t)
        sems = {}
        for name in sorted(semnames):
            sems[name] = nc.alloc_semaphore(name)
        self.sems = sems
        self.nsem = len(sems)
        sched = self

        def emit(ename, eng):
            waited = {}
            for op in sched.ops[ename]:
                need = {}
                for d in op.deps:
                    if d[0] == "E":
                        if d[1] == ename and (ename == "pe" or not SAME_ENGINE_SYNC):
                            continue
                        src = sched.ops[d[1]][d[2]]
                        sn, val = src.sem, src.count
                    else:
                        sn, val = "dma_" + d[1], 16 * d[2]
                    if val > need.get(sn, 0):
                        need[sn] = val
                for sn, val in need.items():
                    if waited.get(sn, 0) >= val:
                        continue
                    eng.wait_ge(sems[sn], val)
                    waited[sn] = val
                if op.fn is None:
                    continue
                inst = op.fn(eng)
                if op.dma_tag is not None:
                    inst.then_inc(sems["dma_" + op.dma_tag], 16)
                elif op.milestone:
                    inst.then_inc(sems[op.sem], 1)

        with nc.Block() as block:
            @block.tensor
            def _(e):
                emit("pe", e)

            @block.scalar
            def _(e):
                emit("act", e)

            @block.vector
            def _(e):
                emit("dve", e)

            @block.gpsimd
            def _(e):
                emit("pool", e)

            @block.sync
            def _(e):
                emit("sp", e)


def build(nseq=2, layers=(0, 1), stop_after=None):
    nc = bass.Bass("TRN2", target_bir_lowering=False)
    S = Sched(nc)

    def din(name, shape):
        return nc.dram_tensor(name, shape, F32, kind="ExternalInput").ap()

    x_d = din("x", [nseq, SEQ, D])
    wA_d = din("wA", [8, 128, 8 * 384])
    rope_d = din("rope", [2, 128, SEQ])
    lam_d = din("lam", [128, 256])
    subg_d = din("subg", [128, 128])
    woA_d = din("woA", [128, 8 * D])
    win_d = din("win", [2, NFC, 128, 8 * 256])
    wout_d = din("wout", [2, 128, NFC * D])
    wB_d = din("wB", [8, 128, 8 * 384])
    bias_d = din("biasB", [8, 128, 2 * 640])
    woB_d = din("woB", [128, 8 * D])
    lng_d = din("lng", [4, 128, D])
    lnb_d = din("lnb", [4, 128, D])
    out_d = nc.dram_tensor("out", [nseq, SEQ, D], F32, kind="ExternalOutput").ap()

    base = (nc.sbuf_base + 63) // 64 * 64
    top = nc.sbuf_top
    cur = [base]

    def dsize(dt):
        return 4 if dt == F32 else 2

    def salloc(name, shape, dt, off=None):
        nb = dsize(dt)
        for s_ in shape[1:]:
            nb *= s_
        if off is None:
            off = cur[0]
            cur[0] += (nb + 63) // 64 * 64
            assert cur[0] <= top, (name, cur[0], top)
        else:
            assert off + nb <= top, (name, off + nb, top)
        return nc.alloc_sbuf_tensor_at(name, list(shape), dt, offset=off)

    ident_bf = salloc("ident_bf", [128, 128], BF16)
    ident_f = salloc("ident_f", [128, 128], F32)
    maskT = salloc("maskT", [128, 64], BF16)
    pswap = salloc("pswap", [128, 128], BF16)
    lamrep = salloc("lamrep", [128, 256], F32)
    lamt = salloc("lamt", [128, 8], F32)
    gs = salloc("gs", [128, 128], F32)
    lng = salloc("lng", [128, D], F32)
    lnb = salloc("lnb", [128, D], F32)
    epst = salloc("epst", [128, 1], F32)
    small = salloc("small", [128, 64], F32)
    X = salloc("X", [128, NT, D], F32)
    r1 = cur[0]
    xT = salloc("xT", [128, 8, SEQ], BF16)
    r2 = cur[0]
    xTb = salloc("xTb", [128, 8, 1024], BF16, off=r1)
    NWIN = 3
    winr = [salloc("win%d" % i, [128, 8, 256], BF16, off=r1 + 16384 + i * 4096) for i in range(NWIN)]
    o = [r2]

    def r2alloc(name, shape, dt):
        nb = dsize(dt)
        for s_ in shape[1:]:
            nb *= s_
        t = salloc(name, shape, dt, off=o[0])
        o[0] += (nb + 63) // 64 * 64
        return t

    oT = r2alloc("oT", [128, 8, SEQ], BF16)
    r2b = o[0]
    QT = r2alloc("QT", [128, SEQ], BF16)
    KT = r2alloc("KT", [128, SEQ], BF16)
    V1 = r2alloc("V1", [128, NT, 130], BF16)
    NET2 = 3
    ET2 = [r2alloc("ET%d" % i, [128, 2, 512], BF16) for i in range(NET2)]
    rope_off = o[0]
    ropeC = r2alloc("ropeC", [128, SEQ], F32)
    ropeS = r2alloc("ropeS", [128, SEQ], F32)
    biasT = salloc("biasT", [128, 2, 640], BF16, off=rope_off)
    wring = [r2alloc("wring%d" % i, [128, 8, 384], BF16) for i in range(2)]
    rt1 = r2alloc("rt1", [128, 512], F32)
    rt2 = r2alloc("rt2", [128, 512], F32)
    qb16 = r2alloc("qb16", [128, 2, 512], BF16)
    nrm_on = r2alloc("nrm_on", [128, 512], BF16)
    accs = r2alloc("accs", [128, 1040], F32)
    att_end = o[0]
    wo_sb = salloc("wo_sb", [128, 8, D], BF16, off=r2b)
    hT = salloc("hT", [128, NFC, 1024], BF16, off=r2)
    wout_sb = salloc("wout_sb", [128, NFC, D], BF16, off=r2 + 49152)
    sgt = [salloc("sgt%d" % i, [128, 512], F32, off=r2 + 49152 + 45056 + i * 2048) for i in range(2)]
    ytmpA = salloc("ytmpA", [128, D], F32, off=r2 + 49152 + 45056 + 4096)
    assert r2 + 49152 + 45056 + 4096 + 4096 <= top, (r2, top)
    assert att_end <= top, (att_end - r2)
    ytmpB = salloc("ytmpB", [128, D], F32, off=r2 + 49152 + 45056)
    ytmp = [ytmpA, ytmpB]

    ps = nc.alloc_psum_tensor("ps", [128, 8 * 512], F32)

    def bank(b, n=512, off=0):
        return ps[:, b * 512 + off: b * 512 + off + n]

    S.add("pool", lambda e: e.memset(ident_f[:, :], 1.0), writes=[("c", "identf")])
    S.add("pool", lambda e: e.affine_select(out=ident_f[:, :], in_=ident_f[:, :], pattern=[[-1, 128]],
                                            compare_op=ALU.is_equal, fill=0.0, base=0, channel_multiplier=1),
          reads=[("c", "identf")], writes=[("c", "identf")])
    S.add("dve", lambda e: e.tensor_copy(out=ident_bf[:, :], in_=ident_f[:, :]), reads=[("c", "identf")],
          writes=[("c", "identb")])
    for (dlo, slo) in ((0, 32), (32, 0), (64, 96), (96, 64)):
        S.add("dve", lambda e, dlo=dlo, slo=slo: e.tensor_copy(out=pswap[:, dlo:dlo + 32], in_=ident_f[:, slo:slo + 32]),
              reads=[("c", "identf")], writes=[("c", "pswap")])
    S.add("dve", lambda e: e.memset(maskT[0:64, :], 0.0), writes=[("c", "mask0")])
    S.add("dve", lambda e: e.memset(maskT[64:128, :], NEG), writes=[("c", "mask1")])
    S.add("dve", lambda e: e.memset(epst[:, :], LN_EPS), writes=[("c", "eps")])
    CONST_KEYS = [("c", "identf"), ("c", "identb"), ("c", "mask0"), ("c", "mask1"), ("c", "eps")]
    S.dma("sp", "cst", lambda e: e.dma_start(out=lamrep[:, :], in_=lam_d[:, :]), writes=[("c", "lamrep")])
    SKIP = os.environ.get("KSKIP", "").split(",")
    if "gs" not in SKIP:
        S.dma("sp", "cst2", lambda e: e.dma_start(out=gs[:, :], in_=subg_d[:, :]), writes=[("c", "gs")])
    S.add("dve", lambda e: e.tensor_tensor(out=lamrep[:, 0:64], in0=lamrep[:, 0:64], in1=lamrep[:, 64:128],
                                           op=ALU.mult), reads=[("c", "lamrep")], writes=[("c", "lamrep")])
    S.add("dve", lambda e: e.tensor_tensor(out=lamrep[:, 128:192], in0=lamrep[:, 128:192], in1=lamrep[:, 192:256],
                                           op=ALU.mult), reads=[("c", "lamrep")], writes=[("c", "lamrep")])
    S.add("dve", lambda e: e.tensor_reduce(out=lamt[:, 0:1], in_=lamrep[:, 0:64], axis=AX.X, op=ALU.add),
          reads=[("c", "lamrep")], writes=[("c", "lamt")])
    S.add("dve", lambda e: e.tensor_reduce(out=lamt[:, 1:2], in_=lamrep[:, 128:192], axis=AX.X, op=ALU.add),
          reads=[("c", "lamrep")], writes=[("c", "lamt")])
    S.add("act", lambda e: e.activation(out=lamt[:, 2:4], in_=lamt[:, 0:2], func=AF.Exp),
          reads=[("c", "lamt")], writes=[("c", "lamt")])
    S.add("dve", lambda e: e.scalar_tensor_tensor(out=lamt[:, 4:5], in0=lamt[:, 3:4], scalar=-LAMBDA_INIT,
                                                  in1=lamt[:, 2:3], op0=ALU.add, op1=ALU.subtract),
          reads=[("c", "lamt")], writes=[("c", "neglam")])
    if "gs" not in SKIP:
        S.add("dve", lambda e: e.tensor_scalar(out=gs[:, :], in0=gs[:, :], scalar1=1.0 - LAMBDA_INIT, scalar2=None,
                                               op0=ALU.mult), reads=[("c", "gs")], writes=[("c", "gs")])

    pe_i = [0]

    def mm(out, lhsT, rhs, start, stop, reads, writes, skip=False):
        S.add("pe", lambda e: e.matmul(out, lhsT, rhs, start=start, stop=stop, skip_group_check=skip),
              reads=reads, writes=writes)

    def tr(out, in_, ident, reads, writes):
        S.add("pe", lambda e: e.transpose(out, in_, ident), reads=reads, writes=writes)

    cp_i = [0]

    def evac(out, in_, reads, writes, scale=None, eng=None):
        cp_i[0] += 1
        writes = list(writes) + [k for k in reads if k[0] == "ps"]
        reads = [k for k in reads if k[0] != "ps"]
        if scale is not None:
            S.add("act", lambda e: e.mul(out, in_, scale), reads=reads, writes=writes)
        elif eng == "act" or (eng is None and cp_i[0] % 2):
            S.add("act", lambda e: e.copy(out=out, in_=in_), reads=reads, writes=writes)
        else:
            S.add("dve", lambda e: e.tensor_copy(out=out, in_=in_), reads=reads, writes=writes)

    def load_x(s):
        if "loadx" in SKIP:
            return
        extra0 = [("D", "ost", S.dma_cnt["ost"])] if S.dma_cnt.get("ost") else []
        for t in range(NT):
            extra = list(extra0)
            if S.dma_cnt.get("ost%d" % t):
                extra.append(("D", "ost%d" % t, S.dma_cnt["ost%d" % t]))
            S.dma("sp", "x%d" % t, lambda e, t=t: e.dma_start(out=X[:, t, :], in_=x_d[s, t * 128:(t + 1) * 128, :]),
                  writes=[("X", t)], extra=extra)

    tb = [0]

    def transposes(tiles, dst, dkey, toff):
        for t in tiles:
            for half in range(2):
                b = tb[0] % 4
                tb[0] += 1
                for c4 in range(4):
                    c = half * 4 + c4
                    tr(bank(b, 128, c4 * 128), X[:, t, c * 128:(c + 1) * 128], ident_f[:, :],
                       reads=[("X", t), ("c", "identf")], writes=[("ps", b)])
                tl = t - toff
                evac(dst[:, half * 4:half * 4 + 4, tl * 128:(tl + 1) * 128],
                     bank(b).rearrange("p (a b) -> p a b", a=4),
                     reads=[("ps", b)], writes=[(dkey, half, t)])

    def xT_keys(dkey, t0, t1):
        return [(dkey, h, t) for h in range(2) for t in range(t0, t1)]

    def ln_parts(t, psrc, psrc_keys, yi, final, s):
        p = yi % 2
        y = ytmp[p]
        yk = ("y", p)
        c0 = 0 if p == 0 else 44

        def sc(a, b):
            return small[:, c0 + a:c0 + b]

        def A():
            S.add("dve", lambda e: e.scalar_tensor_tensor(out=y[:, :], in0=X[:, t, :], scalar=ALPHA, in1=psrc,
                                                          op0=ALU.mult, op1=ALU.add),
                  reads=[("X", t)], writes=[yk] + psrc_keys)
            S.add("dve", lambda e: e.bn_stats(out=sc(0, 6), in_=y[:, 0:512]), reads=[yk], writes=[("st", p, 0)])
            S.add("dve", lambda e: e.bn_stats(out=sc(6, 12), in_=y[:, 512:1024]), reads=[yk],
                  writes=[("st", p, 1)])
            S.add("dve", lambda e: e.bn_aggr(out=sc(12, 14), in_=sc(0, 12)), reads=[("st", p, 0), ("st", p, 1)],
                  writes=[("st", p, 2)])
            S.add("act", lambda e: e.activation(out=sc(14, 15), in_=sc(13, 14), func=AF.Ln,
                                                bias=epst[:, 0:1], scale=1.0),
                  reads=[("st", p, 2), ("c", "eps")], writes=[("st", p, 3)])
            S.add("act", lambda e: e.activation(out=sc(15, 16), in_=sc(14, 15), func=AF.Exp, scale=-0.5),
                  reads=[("st", p, 3)], writes=[("st", p, 4)])

        def B1():
            S.add("dve", lambda e: e.tensor_scalar(out=sc(16, 17), in0=sc(12, 13), scalar1=sc(15, 16),
                                                   scalar2=-1.0, op0=ALU.mult, op1=ALU.mult),
                  reads=[("st", p, 2), ("st", p, 4)], writes=[("st", p, 5)])
            S.add("act", lambda e: e.activation(out=y[:, :], in_=y[:, :], func=AF.Identity,
                                                bias=sc(16, 17), scale=sc(15, 16)),
                  reads=[yk, ("st", p, 4), ("st", p, 5)], writes=[yk])

        def B2():
            S.add("dve", lambda e: e.tensor_tensor(out=y[:, :], in0=y[:, :], in1=lng[:, :], op=ALU.mult),
                  reads=[yk, ("ln", "g")], writes=[yk])
            S.add(LN_ADD_ENG, lambda e: e.tensor_tensor(out=X[:, t, :], in0=y[:, :], in1=lnb[:, :], op=ALU.add),
                  reads=[yk, ("ln", "b")], writes=[("X", t)])
            if final:
                S.dma("sp", "ost%d" % t, lambda e: e.dma_start(out=out_d[s, t * 128:(t + 1) * 128, :], in_=X[:, t, :]),
                      reads=[("X", t)])

        return A, B1, B2

    def layer_norm(t, psrc, psrc_keys, li, yi, final, s):
        A, B1, B2 = ln_parts(t, psrc, psrc_keys, yi, final, s)
        A()
        B1()
        B2()

    def load_ln(li):
        S.dma("sp", "lng", lambda e: e.dma_start(out=lng[:, :], in_=lng_d[li]), writes=[("ln", "g")])
        S.dma("sp", "lnb", lambda e: e.dma_start(out=lnb[:, :], in_=lnb_d[li]), writes=[("ln", "b")])

    wr_i = [0]
    et_i = [0]
    sb_i = [0]

    def attention(s, layer):
        isA = layer == 0
        W = 129 if isA else 65
        per_bank = 512 // W

        def acc(m, q4):
            if isA:
                if q4 < 3:
                    b, col = 4 + m, q4 * W
                else:
                    b, col = 6, m * W
            else:
                b, col = 4 + m, q4 * W
            return bank(b, W, col), None, b

        accs_t = accs.tensor if hasattr(accs, "tensor") else accs
        pstep = accs[:, :].ap[0][0]

        def accs_ap(off, dims):
            return bass.AP(accs_t, off, [[pstep, 128]] + [list(d_) for d_ in dims])

        started = set()
        pending = []
        NDEFER = int(os.environ.get('KNDEFER', '8')) if isA else 4
        NS2 = int(os.environ.get('KNS2', '6'))

        if isA:
            S.dma("sp", "rope", lambda e: e.dma_start(out=ropeC[:, :], in_=rope_d[0]), writes=[("rope", 0)])
            S.dma("sp", "rope", lambda e: e.dma_start(out=ropeS[:, :], in_=rope_d[1]), writes=[("rope", 1)])
            S.lastw[("rope", 0)] = S.lastw[("rope", 1)]
        if isA:
            S.add("dve", lambda e: e.memset(V1[:, :, 128:129], 1.0), writes=[("V1ones",)])
        else:
            S.add("dve", lambda e: e.memset(V1[:, :, 64:65], 1.0), writes=[("V1ones",)])
            S.add("dve", lambda e: e.memset(V1[:, :, 129:130], 1.0), reads=[("V1ones",)], writes=[("V1ones",)])
        ncol = 384
        wsrc = wA_d if isA else wB_d
        for u in range(int(os.environ.get('KUNITS', '8'))):
            slot = wr_i[0] % 2
            wr_i[0] += 1
            wk = ("wr", slot)
            wt = wring[slot]
            S.dma("pool", "wr%d" % slot,
                  lambda e, u=u, wt=wt: e.dma_start(out=wt[:, :, 0:ncol],
                                                    in_=wsrc[u].rearrange("p (a b) -> p a b", a=8)),
                  writes=[wk])
            if not isA:
                S.dma("pool", "bias", lambda e, u=u: e.dma_start(out=biasT[:, :, :],
                                                                 in_=bias_d[u].rearrange("p (a b) -> p a b", a=2)),
                      writes=[("biasT",)])
                S.add("dve", lambda e: e.memset(biasT[64:128, :, 0:64], NEG), reads=[("biasT",)], writes=[("biasT",)])
                S.add("dve", lambda e: e.memset(biasT[0:64, :, 576:640], NEG), reads=[("biasT",)], writes=[("biasT",)])
            for p_ in [p_ for p_ in pending if p_[2] == "s2"]:
                p_[1]()
                pending.remove(p_)
            vc = 256
            for t4 in range(4):
                vb = 7 - t4
                for tt in range(4):
                    t = t4 * 4 + tt
                    for k in range(8):
                        mm(bank(vb, 128, tt * 128), xT[:, k, t * 128:(t + 1) * 128], wt[:, k, vc:vc + 128],
                           k == 0, k == 7, reads=[wk] + ([("xT", 0, t), ("xT", 1, t)] if k == 0 else []),
                           writes=[("ps", vb)])
                vkeys = [("V1", t4 * 4 + tt) for tt in range(4)]
                if isA:
                    evac(V1[:, t4 * 4:t4 * 4 + 4, 0:128], bank(vb).rearrange("p (a b) -> p a b", a=4),
                         reads=[("ps", vb), ("V1ones",)], writes=vkeys)
                else:
                    for m in range(2):
                        evac(V1[:, t4 * 4:t4 * 4 + 4, m * 65:m * 65 + 64],
                             bank(vb).rearrange("p (a b) -> p a b", a=4)[:, :, m * 64:(m + 1) * 64],
                             reads=[("ps", vb), ("V1ones",)], writes=vkeys)
            for j in range(4):
                tsl = slice(j * 512, (j + 1) * 512)
                xk = xT_keys("xT", j * 4, j * 4 + 4)
                pbase = (j % 2) * 4 if isA else (j % 2) * 2
                for pi in range(2):
                    for k in range(8):
                        mm(bank(pbase + pi), wt[:, k, pi * 128:(pi + 1) * 128], xT[:, k, tsl], k == 0, k == 7,
                           reads=[wk] + (xk if k == 0 else []), writes=[("ps", pbase + pi)])
                if isA:
                    for pi in range(2):
                        b = pbase + pi
                        S.add("act", lambda e, pi=pi, b=b: e.copy(out=qb16[:, pi, :], in_=bank(b)),
                              reads=[], writes=[("qb16", pi), ("ps", b)])
                    for pi in range(2):
                        mm(bank(pbase + 2 + pi), pswap[:, :], qb16[:, pi, :], True, True,
                           reads=[("qb16", pi), ("c", "pswap")], writes=[("ps", pbase + 2 + pi)])
                    for (pi, dst, dk) in ((0, QT, "QT"), (1, KT, "KT")):
                        b = pbase + pi
                        S.add("dve", lambda e, b=b, tsl=tsl: e.tensor_tensor(out=rt1[:, :], in0=bank(b),
                                                                             in1=ropeC[:, tsl], op=ALU.mult),
                              reads=[("rope", 0)], writes=[("rt", 1), ("ps", b)])
                        S.add("dve", lambda e, b=b, tsl=tsl: e.tensor_tensor(out=rt2[:, :], in0=bank(b + 2),
                                                                             in1=ropeS[:, tsl], op=ALU.mult),
                              reads=[("rope", 0)], writes=[("rt", 2), ("ps", b + 2)])
                        S.add("dve", lambda e, dst=dst, tsl=tsl: e.tensor_tensor(
                            out=dst[:, tsl], in0=rt1[:, :], in1=rt2[:, :], op=ALU.add),
                            reads=[("rt", 1), ("rt", 2)], writes=[(dk, j)])
                else:
                    evac(QT[:, tsl], bank(pbase), reads=[("ps", pbase)], writes=[("QT", j)], scale=0.125)
                    evac(KT[:, tsl], bank(pbase + 1), reads=[("ps", pbase + 1)], writes=[("KT", j)])
            steps = []
            for qb in range(4):
                kts = list(range(0, 4 * qb + 4)) if isA else list(range(max(0, 4 * qb - 4), 4 * qb + 4))
                for kt in kts:
                    steps.append((qb, kt, kt == kts[-1]))

            def emit_scores(qb, kt):
                qlo = max(512 * qb, 128 * kt)
                qhi = 512 * (qb + 1) if isA else min(512 * (qb + 1), 128 * kt + 640)
                n = qhi - qlo
                ksl = slice(kt * 128, (kt + 1) * 128)
                qkeys = [("QT", qb)]
                kkeys = [("KT", kt // 4)]
                ets = []
                b0 = sb_i[0] % 4
                ei = et_i[0] % NET2
                et_i[0] += 1
                et2 = ET2[ei]
                ek = ("ET", ei)
                for m in range(2):
                    b = sb_i[0] % 4
                    sb_i[0] += 1
                    psl = slice(m * 64, (m + 1) * 64)
                    pk = ("ps", b)
                    if isA:
                        if qlo == 128 * kt:
                            mm(bank(b, 64), ident_bf[:, :], maskT[:, :], True, False,
                               reads=CONST_KEYS, writes=[pk])
                        else:
                            mm(bank(b, n), KT[psl, ksl], QT[psl, qlo:qhi], True, True,
                               reads=qkeys + kkeys, writes=[pk])
                    else:
                        mm(bank(b, n), ident_bf[:, :], biasT[:, m, qlo - 128 * kt:qhi - 128 * kt], True, False,
                           reads=CONST_KEYS + [("biasT",)], writes=[pk])
                    ets.append((et2[:, m, :], ek))
                if isA and qlo == 128 * kt:
                    for m in range(2):
                        psl = slice(m * 64, (m + 1) * 64)
                        mm(bank(b0 + m, 64), KT[psl, ksl], QT[psl, qlo:qlo + 64], False, True,
                           reads=qkeys + kkeys, writes=[("ps", b0 + m)])
                    for m in range(2):
                        psl = slice(m * 64, (m + 1) * 64)
                        mm(bank(b0 + m, n - 64, 64), KT[psl, ksl], QT[psl, qlo + 64:qhi], True, True,
                           reads=[], writes=[("ps", b0 + m)])
                if not isA:
                    for m in range(2):
                        psl = slice(m * 64, (m + 1) * 64)
                        mm(bank(b0 + m, n), KT[psl, ksl], QT[psl, qlo:qhi], False, True,
                           reads=qkeys + kkeys, writes=[("ps", b0 + m)])
                S.add("act", lambda e, et2=et2, b0=b0, n=n: e.activation(
                    out=et2[:, :, 0:n], in_=ps[:, b0 * 512:(b0 + 2) * 512].rearrange("p (a b) -> p a b", a=2)[:, :, 0:n],
                    func=AF.Exp, scale=(0.125 if isA else 1.0)),
                    reads=[], writes=[ek, ("ps", b0), ("ps", b0 + 1)])
                return (qlo, qhi, ets)

            def emit_pv(qb, kt, info):
                qlo, qhi, ets = info
                for m in range(2):
                    et, ek = ets[m]
                    for qt in range(qlo // 128, qhi // 128):
                        last = kt == qt
                        a_ap, a_k, a_b = acc(m, qt % 4)
                        first = (u, qb, a_b) not in started
                        started.add((u, qb, a_b))
                        if isA:
                            rhs = V1[:, kt, 0:129]
                        else:
                            rhs = V1[:, kt, m * 65:m * 65 + 65]
                        off = qt * 128 - qlo
                        mm(a_ap, et[:, off:off + 128], rhs, first, last,
                           reads=[ek, ("V1", kt)], writes=[("ps", a_b)], skip=True)

            def emit_norm(qb):
                AK = [("accs", 0)]
                if isA:
                    S.add("dve", lambda e: e.tensor_copy(out=accs[:, 0:3 * W], in_=bank(4, 3 * W)),
                          writes=AK + [("ps", 4)])
                    S.add("dve", lambda e: e.tensor_copy(out=accs_ap(3 * W, [[4 * W, 2], [1, W]]),
                                                        in_=bank(6, 2 * W).rearrange("p (a b) -> p a b", a=2)),
                          writes=AK + [("ps", 6)])
                    S.add("dve", lambda e: e.tensor_copy(out=accs[:, 4 * W:7 * W], in_=bank(5, 3 * W)),
                          writes=AK + [("ps", 5)])
                else:
                    S.add("dve", lambda e: e.tensor_copy(out=accs[:, 0:4 * W], in_=bank(4, 4 * W)),
                          writes=AK + [("ps", 4)])
                    S.add("dve", lambda e: e.tensor_copy(out=accs[:, 4 * W:8 * W], in_=bank(5, 4 * W)),
                          writes=AK + [("ps", 5)])
                Wv = W - 1
                S.add("dve", lambda e: e.reciprocal(out=small[:, 20:28], in_=accs_ap(Wv, [[W, 8]])),
                      reads=AK, writes=[("nr", 0)])

                def rbc(m):
                    return small[:, 20 + 4 * m:24 + 4 * m].unsqueeze(2).broadcast_to([128, 4, Wv])

                def av(m):
                    return accs_ap(m * 4 * W, [[W, 4], [1, Wv]])

                onk = ("on",)
                if isA:
                    t4 = rt1[:, :].rearrange("p (a b) -> p a b", a=4)
                    u4 = rt2[:, :].rearrange("p (a b) -> p a b", a=4)
                    S.add("dve", lambda e: e.tensor_tensor(out=t4, in0=av(1), in1=rbc(1), op=ALU.mult),
                          reads=AK + [("nr", 0)], writes=[("rt", 1)])
                    S.add("dve", lambda e: e.tensor_tensor(out=u4, in0=av(0), in1=rbc(0), op=ALU.mult),
                          reads=AK + [("nr", 0)], writes=[("rt", 2)])
                    S.add("dve", lambda e: e.scalar_tensor_tensor(out=rt1[:, :], in0=rt1[:, :], scalar=lamt[:, 4:5],
                                                                  in1=rt2[:, :], op0=ALU.mult, op1=ALU.add),
                          reads=[("rt", 2), ("c", "neglam")], writes=[("rt", 1)])
                    S.add("dve", lambda e: e.tensor_tensor(out=rt2[:, :], in0=rt1[:, :], in1=rt1[:, :], op=ALU.mult),
                          reads=[("rt", 1)], writes=[("rt", 2)])
                    S.add("dve", lambda e: e.tensor_reduce(out=small[:, 28:32], in_=u4, axis=AX.X, op=ALU.add),
                          reads=[("rt", 2)], writes=[("nr", 2)])
                    S.add("dve", lambda e: e.tensor_scalar(out=small[:, 32:36], in0=small[:, 28:32],
                                                           scalar1=1.0 / 128.0, scalar2=LN_EPS,
                                                           op0=ALU.mult, op1=ALU.add),
                          reads=[("nr", 2)], writes=[("nr", 3)])
                else:
                    on4 = nrm_on[:, :].rearrange("p (a b) -> p a b", a=4)
                    for m in range(2):
                        S.add("dve", lambda e, m=m: e.tensor_tensor(out=on4[:, :, m * 64:(m + 1) * 64], in0=av(m),
                                                                    in1=rbc(m), op=ALU.mult),
                              reads=AK + [("nr", 0)], writes=[onk])

            def emit_norm2(qb):
                t4 = rt1[:, :].rearrange("p (a b) -> p a b", a=4)
                onk = ("on",)
                S.add("act", lambda e: e.activation(out=small[:, 36:40], in_=small[:, 32:36], func=AF.Ln),
                      reads=[("nr", 3)], writes=[("nr", 4)])
                S.add("act", lambda e: e.activation(out=small[:, 40:44], in_=small[:, 36:40], func=AF.Exp,
                                                    scale=-0.5), reads=[("nr", 4)], writes=[("nr", 5)])
                S.add("dve", lambda e: e.tensor_tensor(
                    out=t4, in0=t4, in1=small[:, 40:44].unsqueeze(2).broadcast_to([128, 4, 128]), op=ALU.mult),
                    reads=[("nr", 5)], writes=[("rt", 1)])
                S.add("dve", lambda e: e.tensor_tensor(
                    out=nrm_on[:, :].rearrange("p (a b) -> p a b", a=4), in0=t4,
                    in1=gs[:, :].unsqueeze(1).broadcast_to([128, 4, 128]), op=ALU.mult),
                    reads=[("rt", 1), ("c", "gs")], writes=[onk])

            def emit_norm_pe(qb, u):
                for q4 in range(4):
                    tr(bank(7, 64, q4 * 64).bitcast(BF16), nrm_on[:, q4 * 128:(q4 + 1) * 128], ident_bf[:, :],
                       reads=[("on",), ("c", "identb")], writes=[("ps", 7)])
                evac(oT[:, u, qb * 512:(qb + 1) * 512], bank(7, 256).bitcast(BF16),
                     reads=[("ps", 7)], writes=[("oT", u, qb)])

            infos = {}
            if os.environ.get('KSTEPS'):
                steps = steps[:int(os.environ['KSTEPS'])]
            for i in range(len(steps) + 1):
                if i < len(steps):
                    infos[i] = emit_scores(steps[i][0], steps[i][1])
                if i >= 1:
                    qb, kt, lastk = steps[i - 1]
                    emit_pv(qb, kt, infos.pop(i - 1))
                    for p_ in list(pending):
                        p_[0] -= 1
                        if p_[0] <= 0:
                            p_[1]()
                            pending.remove(p_)
                    if lastk and not os.environ.get('KNONORM'):
                        for p_ in pending:
                            p_[1]()
                        del pending[:]
                        emit_norm(qb)
                        if isA:
                            pending.append([NS2, (lambda qb=qb, f=emit_norm2: f(qb)), "s2"])
                        pending.append([NDEFER, (lambda qb=qb, u=u, f=emit_norm_pe: f(qb, u)), "pe"])

        for p_ in pending:
            p_[1]()
        del pending[:]

    def out_proj(s, layer):
        wsrc = woA_d if layer == 0 else woB_d
        S.dma("pool", "wo", lambda e: e.dma_start(out=wo_sb[:, :, :], in_=wsrc.rearrange("p (a b) -> p a b", a=8)),
              writes=[("wo",)])
        load_ln(layer * 2)
        prev = None
        for t in range(NT):
            pb = (t % 2) * 2
            for fo in range(2):
                for c in range(8):
                    mm(bank(pb + fo), oT[:, c, t * 128:(t + 1) * 128], wo_sb[:, c, fo * 512:(fo + 1) * 512],
                       c == 0, c == 7, reads=[("wo",), ("oT", c, t // 4)], writes=[("ps", pb + fo)])
            A, B1, B2 = ln_parts(t, ps[:, pb * 512:(pb + 2) * 512], [("ps", pb), ("ps", pb + 1)], t, False, s)
            if prev is not None:
                prev[0]()
            A()
            if prev is not None:
                prev[1]()
            prev = (B1, B2)
        prev[0]()
        prev[1]()

    wi_i = [0]
    fb_i = [0]

    def ffn(s, layer, final):
        toks = []
        for q in range(4):
            c0 = q * 6
            c1 = min(NFC, c0 + 6)
            toks.append(S.dma("pool", "wout%d" % q,
                              lambda e, c0=c0, c1=c1: e.dma_start(
                                  out=wout_sb[:, c0:c1, :],
                                  in_=wout_d[layer][:, c0 * D:c1 * D].rearrange("p (a b) -> p a b", b=D)),
                              writes=[("wout", c) for c in range(c0, c1)]))
        load_ln(layer * 2 + 1)
        for blk in range(2):
            t0 = blk * 8
            transposes(range(t0, t0 + 8), xTb, "xTb", t0)
            for c in range(NFC):
                slot = wi_i[0] % NWIN
                wi_i[0] += 1
                wt = winr[slot]
                wk = ("win", slot)
                S.dma("pool", "win%d" % slot,
                      lambda e, c=c, wt=wt: e.dma_start(out=wt[:, :, :],
                                                        in_=win_d[layer, c].rearrange("p (a b) -> p a b", a=8)),
                      writes=[wk])
                for half in range(2):
                    pb = (fb_i[0] % 2) * 2
                    fb_i[0] += 1
                    tsl = slice(half * 512, (half + 1) * 512)
                    xk = xT_keys("xTb", t0 + half * 4, t0 + half * 4 + 4)
                    for gu in range(2):
                        for k in range(8):
                            mm(bank(pb + gu), wt[:, k, gu * 128:(gu + 1) * 128], xTb[:, k, tsl], k == 0, k == 7,
                               reads=[wk] + (xk if k == 0 else []), writes=[("ps", pb + gu)])
                    sg = sgt[half]
                    S.add("act", lambda e, sg=sg, pb=pb: e.activation(out=sg[:, :], in_=bank(pb), func=AF.Silu),
                          reads=[], writes=[("sg", half), ("ps", pb)])
                    S.add("dve", lambda e, sg=sg, pb=pb, c=c, tsl=tsl: e.tensor_tensor(out=hT[:, c, tsl], in0=sg[:, :],
                                                                              in1=bank(pb + 1), op=ALU.mult),
                          reads=[("sg", half)], writes=[("hT", c, half), ("ps", pb + 1)])
            for tl in range(8):
                t = t0 + tl
                pb = 4 + (tl % 2) * 2
                for fo in range(2):
                    for c in range(NFC):
                        mm(bank(pb + fo), hT[:, c, tl * 128:(tl + 1) * 128], wout_sb[:, c, fo * 512:(fo + 1) * 512],
                           c == 0, c == NFC - 1, reads=[("wout", c), ("hT", c, tl // 4)], writes=[("ps", pb + fo)])
                layer_norm(t, ps[:, pb * 512:(pb + 2) * 512], [("ps", pb), ("ps", pb + 1)], layer * 2 + 1, 0,
                           final, s)

    def dump_oT(s):
        kd = os.environ.get('KDUMP', 'oT')
        if kd == 'oT':
            S.dma("pool", "ost", lambda e: e.dma_start(
                out=out_d[s].rearrange("(p a) d -> p (a d)", p=128),
                in_=oT[:, :, :].rearrange("p a b -> p (a b)")),
                reads=[("oT", u, qb) for u in range(8) for qb in range(4)])
        elif kd == 'QK':
            S.dma("pool", "ost", lambda e: e.dma_start(
                out=out_d[s].rearrange("(p a) d -> p (a d)", p=128)[:, 0:2048], in_=QT[:, :]),
                reads=[("QT", j) for j in range(4)])
            S.dma("pool", "ost", lambda e: e.dma_start(
                out=out_d[s].rearrange("(p a) d -> p (a d)", p=128)[:, 2048:4096], in_=KT[:, :]),
                reads=[("KT", j) for j in range(4)])
            S.dma("pool", "ost", lambda e: e.dma_start(
                out=out_d[s].rearrange("(p a) d -> p (a d)", p=128)[:, 4096:4096 + 16 * 130],
                in_=V1[:, :, :].rearrange("p a b -> p (a b)")),
                reads=[("V1", j) for j in range(16)])

    def dump_x(s):
        for t in range(NT):
            S.dma("sp", "ost", lambda e, t=t: e.dma_start(out=out_d[s, t * 128:(t + 1) * 128, :], in_=X[:, t, :]),
                  reads=[("X", t)])

    last_store = None
    for s in range(nseq):
        S.new_epoch()
        load_x(s)
        done = False
        for layer in layers:
            if stop_after == ("init",):
                break
            S.new_epoch()
            transposes(range(NT), xT, "xT", 0)
            if stop_after == ("xT",):
                S.dma("pool", "ost", lambda e: e.dma_start(
                    out=out_d[s].rearrange("(p a) d -> p (a d)", p=128),
                    in_=xT[:, :, :].rearrange("p a b -> p (a b)")),
                    reads=xT_keys("xT", 0, NT))
                break
            attention(s, layer)
            S.set_fence(["wo", "y", "wout", "xTb", "win", "sg"], S.snapshot())
            if stop_after == ("attn", layer):
                done = True
                break
            S.new_epoch()
            out_proj(s, layer)
            S.set_fence(["hT", "sg"], S.snapshot())
            if stop_after == ("op", layer):
                done = True
                break
            S.new_epoch()
            final = (layer == layers[-1]) and stop_after is None
            ffn(s, layer, final)
            S.set_fence(["xT", "wr", "QT", "KT", "V1", "V1ones", "ET", "rope", "biasT", "rt",                          "on", "oT", "accs", "qb16"], S.snapshot())
            if stop_after == ("ffn", layer):
                done = True
                break
        if stop_after == ("xT",):
            pass
        elif stop_after is not None:
            if stop_after[0] == "attn":
                dump_oT(s)
            else:
                dump_x(s)
    S.wait_only("sp", [("D", tag, cnt) for tag, cnt in S.dma_cnt.items() if tag.startswith("ost")])
    S.finalize()
    return nc, S


def prep_inputs(x, a_w_qkv, a_lambda, a_subln_g, a_w_o, kv_w, b_w_q, b_rel_bias, b_w_o,
                ln_g, ln_b, ffn_w_in, ffn_w_out):
    f = np.float32
    wqkv = np.asarray(a_w_qkv, f)[0]
    wA = np.empty((8, 128, 8, 384), f)
    for h in range(8):
        qc = np.arange(h * 128, (h + 1) * 128)
        cols = np.concatenate([qc, 1024 + qc, 2048 + qc])
        blk = wqkv[:, cols]
        wA[h] = blk.reshape(8, 128, 384).transpose(1, 0, 2)
    wA = np.ascontiguousarray(wA.reshape(8, 128, 8 * 384))
    inv = 10000.0 ** (-np.arange(0, 64, 2, dtype=np.float64) / 64.0)
    ang = np.arange(SEQ, dtype=np.float64)[:, None] * inv[None, :]
    cos = np.cos(ang).astype(f).T
    sin = np.sin(ang).astype(f).T
    C = np.concatenate([cos, cos, cos, cos], axis=0)
    Sg = np.concatenate([-sin, sin, -sin, sin], axis=0)
    rope = np.ascontiguousarray(np.stack([C, Sg]).astype(f))
    lam = np.ascontiguousarray(np.broadcast_to(np.asarray(a_lambda, f)[0].reshape(1, 256), (128, 256)))
    subg = np.ascontiguousarray(np.broadcast_to(np.asarray(a_subln_g, f)[0].reshape(1, 128), (128, 128)))

    def kmajor(w, nk):
        n = w.shape[1]
        return np.ascontiguousarray(w.reshape(nk, 128, n).transpose(1, 0, 2).reshape(128, nk * n))

    woA = kmajor(np.asarray(a_w_o, f)[0], 8)
    woB = kmajor(np.asarray(b_w_o, f)[0], 8)
    win = np.empty((2, NFC, 128, 8 * 256), f)
    w_in = np.asarray(ffn_w_in, f)
    for l in range(2):
        for c in range(NFC):
            cols = np.concatenate([np.arange(c * 128, (c + 1) * 128), DFF + np.arange(c * 128, (c + 1) * 128)])
            win[l, c] = kmajor(w_in[l][:, cols], 8)
    wout = np.stack([kmajor(np.asarray(ffn_w_out, f)[l], NFC) for l in range(2)])
    kvw = np.asarray(kv_w, f)
    wq = np.asarray(b_w_q, f)[0]
    wB = np.empty((8, 128, 8 * 384), f)
    for p in range(8):
        pc = np.arange(p * 128, (p + 1) * 128)
        blk = np.concatenate([wq[:, pc], kvw[:, pc], kvw[:, 1024 + pc]], axis=1)
        wB[p] = kmajor(blk, 8)
    tab = np.asarray(b_rel_bias, f)[0]
    i = np.arange(128)[:, None]
    j = np.arange(640)[None, :]
    idx = np.clip(i - j, -256, 256) + 256
    bt = tab[:, idx]
    biasB = np.ascontiguousarray(bt.reshape(8, 2, 128, 640).transpose(0, 2, 1, 3).reshape(8, 128, 1280))
    g = np.asarray(ln_g, f).reshape(4, 1, D)
    b = np.asarray(ln_b, f).reshape(4, 1, D)
    lng = np.ascontiguousarray(np.broadcast_to(g, (4, 128, D)))
    lnb = np.ascontiguousarray(np.broadcast_to(b, (4, 128, D)))
    shared = dict(wA=wA, rope=rope, lam=lam, subg=subg, woA=woA, win=win, wout=np.ascontiguousarray(wout),
                  wB=wB, biasB=biasB, woB=woB, lng=lng, lnb=lnb)
    return shared


_CACHE = {}


def kernel(x, a_w_qkv, a_lambda, a_subln_g, a_w_o, kv_w, b_w_q, b_rel_bias, b_w_o,
           ln_g, ln_b, ffn_w_in, ffn_w_out):
    shared = prep_inputs(x, a_w_qkv, a_lambda, a_subln_g, a_w_o, kv_w, b_w_q, b_rel_bias, b_w_o,
                         ln_g, ln_b, ffn_w_in, ffn_w_out)
    x = np.asarray(x, np.float32)
    if "nc" not in _CACHE:
        _CACHE["nc"] = build(nseq=2)[0]
    nc = _CACHE["nc"]
    in_maps = []
    for c in range(8):
        m = dict(shared)
        m["x"] = np.ascontiguousarray(x[2 * c:2 * c + 2])
        in_maps.append(m)
    res = run_bass_kernel_spmd(nc, in_maps, core_ids=list(range(8)))
    out = np.concatenate([np.asarray(r["out"], np.float32) for r in res.results], axis=0)
    return out.reshape(16, SEQ, D)
```
